# Optimizing a Trainium2 kernel written in Bass

```python
import jax, jax.numpy as jnp
from jax import lax
import numpy as np

D_MODEL = 1024
BATCH = 4
SEQ = 8192
DEPTH = 2

CHUNK = 64
D_MIX = D_MODEL
D_LRU = D_MIX // 2
D_ATTN = D_MIX - D_LRU
HEAD_DIM = 64
N_ATTN_HEADS = D_ATTN // HEAD_DIM
LRU_BLOCKS = 8
LRU_BLOCK_W = D_LRU // LRU_BLOCKS
CONV_W = 4
LRU_C = 8.0
Q_BLOCK = 128
N_IN = 2 * D_LRU + 4 * D_ATTN + N_ATTN_HEADS
DEEPNORM_ALPHA = (2.0 * DEPTH) ** 0.25
DEEPNORM_BETA = (8.0 * DEPTH) ** -0.25
LN_EPS = 1e-5
NEG_INF = -1e30

kernel_name = "hymba_rglru_fox_adaln_deepnorm"


def _layer_norm(x):
    x32 = x.astype(jnp.float32)
    mu = jnp.mean(x32, axis=-1, keepdims=True)
    var = jnp.mean(jnp.square(x32 - mu), axis=-1, keepdims=True)
    return (x32 - mu) * lax.rsqrt(var + LN_EPS)


def _rms_norm(y, g):
    y32 = y.astype(jnp.float32)
    return y32 * lax.rsqrt(jnp.mean(jnp.square(y32), axis=-1, keepdims=True) + LN_EPS) * g.astype(jnp.float32)


def _causal_depthwise_conv(x, w, b):
    K = w.shape[0]
    S = x.shape[1]
    xp = jnp.pad(x, ((0, 0), (K - 1, 0), (0, 0)))
    out = b
    for k in range(K):
        out = out + xp[:, k:k + S, :] * w[k]
    return out


def _rg_lru(x, w_a, b_a, w_x, b_x, lam):
    B, S, C = x.shape
    xb = x.reshape(B, S, LRU_BLOCKS, LRU_BLOCK_W)
    r = jax.nn.sigmoid((jnp.einsum('bsnd,nde->bsne', xb, w_a).reshape(B, S, C) + b_a).astype(jnp.float32))
    i = jax.nn.sigmoid((jnp.einsum('bsnd,nde->bsne', xb, w_x).reshape(B, S, C) + b_x).astype(jnp.float32))
    log_a = -LRU_C * r * jax.nn.softplus(-lam.astype(jnp.float32))
    a = jnp.exp(log_a)
    mult = jnp.sqrt(-jnp.expm1(2.0 * log_a))
    u = mult * (i * x.astype(jnp.float32))

    def combine(left, right):
        a1, b1 = left
        a2, b2 = right
        return a2 * a1, a2 * b1 + b2

    _, h = lax.associative_scan(combine, (a, u), axis=1)
    return h


def _forgetting_attention(q, k, v, log_f):
    B, S, _ = q.shape
    H, dh = N_ATTN_HEADS, HEAD_DIM
    q = q.reshape(B, S, H, dh).transpose(0, 2, 1, 3)
    k = k.reshape(B, S, H, dh).transpose(0, 2, 1, 3)
    v = v.reshape(B, S, H, dh).transpose(0, 2, 1, 3)
    d_cum = jnp.cumsum(log_f.astype(jnp.float32), axis=1).transpose(0, 2, 1)
    scale = dh ** -0.5
    kpos = jnp.arange(S)
    n_blocks = S // Q_BLOCK

    def block(bi):
        qs = bi * Q_BLOCK
        qb = lax.dynamic_slice_in_dim(q, qs, Q_BLOCK, axis=2)
        dq = lax.dynamic_slice_in_dim(d_cum, qs, Q_BLOCK, axis=2)
        s = jnp.einsum('bhqd,bhkd->bhqk', qb, k).astype(jnp.float32) * scale
        s = s + (dq[..., :, None] - d_cum[:, :, None, :])
        qpos = qs + jnp.arange(Q_BLOCK)
        mask = kpos[None, :] <= qpos[:, None]
        s = jnp.where(mask, s, NEG_INF)
        p = jax.nn.softmax(s, axis=-1)
        return jnp.einsum('bhqk,bhkd->bhqd', p.astype(v.dtype), v)

    out = lax.map(block, jnp.arange(n_blocks))
    return out.transpose(1, 0, 3, 2, 4).reshape(B, S, H * dh)


def setup_inputs(seed: int = 0) -> dict:
    key = jax.random.key(seed)
    ks = jax.random.split(key, 18)
    f32 = jnp.float32
    x = jax.random.normal(ks[0], (BATCH, SEQ, D_MODEL), f32)
    c = jax.random.normal(ks[1], (BATCH, D_MODEL), f32)
    w_ada = jax.random.normal(ks[2], (DEPTH, D_MODEL, 3 * D_MODEL), f32) * (0.5 * D_MODEL ** -0.5)
    b_ada = 0.01 * jax.random.normal(ks[3], (DEPTH, 3 * D_MODEL), f32)
    w_in = jax.random.normal(ks[4], (DEPTH, D_MODEL, N_IN), f32) * D_MODEL ** -0.5
    b_fgate = jax.random.uniform(ks[5], (DEPTH, N_ATTN_HEADS), f32, minval=1.0, maxval=5.0)
    conv_w = jax.random.normal(ks[6], (DEPTH, CONV_W, D_LRU), f32) * CONV_W ** -0.5
    conv_b = 0.01 * jax.random.normal(ks[7], (DEPTH, D_LRU), f32)
    w_gate_a = jax.random.normal(ks[8], (DEPTH, LRU_BLOCKS, LRU_BLOCK_W, LRU_BLOCK_W), f32) * LRU_BLOCK_W ** -0.5
    b_gate_a = 0.01 * jax.random.normal(ks[9], (DEPTH, D_LRU), f32)
    w_gate_x = jax.random.normal(ks[10], (DEPTH, LRU_BLOCKS, LRU_BLOCK_W, LRU_BLOCK_W), f32) * LRU_BLOCK_W ** -0.5
    b_gate_x = 0.01 * jax.random.normal(ks[11], (DEPTH, D_LRU), f32)
    u = jax.random.uniform(ks[12], (DEPTH, D_LRU), f32, minval=0.9, maxval=0.999)
    a0 = u ** (1.0 / LRU_C)
    lru_lambda = jnp.log(a0) - jnp.log1p(-a0)
    norm_lru = 1.0 + 0.01 * jax.random.normal(ks[13], (DEPTH, D_LRU), f32)
    norm_attn = 1.0 + 0.01 * jax.random.normal(ks[14], (DEPTH, D_ATTN), f32)
    w_out = jax.random.normal(ks[15], (DEPTH, D_MIX, D_MODEL), f32) * (D_MIX ** -0.5 * DEEPNORM_BETA)
    ln_gain = 1.0 + 0.01 * jax.random.normal(ks[16], (DEPTH, D_MODEL), f32)
    ln_bias = 0.01 * jax.random.normal(ks[17], (DEPTH, D_MODEL), f32)
    return {"x": x, "c": c, "w_ada": w_ada, "b_ada": b_ada, "w_in": w_in, "b_fgate": b_fgate,
            "conv_w": conv_w, "conv_b": conv_b, "w_gate_a": w_gate_a, "b_gate_a": b_gate_a,
            "w_gate_x": w_gate_x, "b_gate_x": b_gate_x, "lru_lambda": lru_lambda,
            "norm_lru": norm_lru, "norm_attn": norm_attn, "w_out": w_out,
            "ln_gain": ln_gain, "ln_bias": ln_bias}


def reference(x, c, w_ada, b_ada, w_in, b_fgate, conv_w, conv_b, w_gate_a, b_gate_a,
              w_gate_x, b_gate_x, lru_lambda, norm_lru, norm_attn, w_out, ln_gain, ln_bias):
    dtype = x.dtype
    split_idx = [D_LRU, 2 * D_LRU, 2 * D_LRU + D_ATTN, 2 * D_LRU + 2 * D_ATTN,
                 2 * D_LRU + 3 * D_ATTN, 2 * D_LRU + 4 * D_ATTN]
    c_act = jax.nn.silu(c)
    for l in range(DEPTH):
        mod = c_act @ w_ada[l] + b_ada[l]
        shift, scale, gate = jnp.split(mod, 3, axis=-1)
        h = (_layer_norm(x) * (1.0 + scale[:, None, :].astype(jnp.float32))
             + shift[:, None, :].astype(jnp.float32)).astype(dtype)
        proj = h @ w_in[l]
        xr, zr, q, k, v, za, fl = jnp.split(proj, split_idx, axis=-1)
        xc = _causal_depthwise_conv(xr, conv_w[l], conv_b[l])
        y_r = _rg_lru(xc, w_gate_a[l], b_gate_a[l], w_gate_x[l], b_gate_x[l], lru_lambda[l])
        y_r = (_rms_norm(y_r, norm_lru[l]) * jax.nn.silu(zr.astype(jnp.float32))).astype(dtype)
        log_f = jax.nn.log_sigmoid((fl + b_fgate[l]).astype(jnp.float32))
        y_a = _forgetting_attention(q, k, v, log_f)
        y_a = (_rms_norm(y_a, norm_attn[l]) * jax.nn.silu(za.astype(jnp.float32))).astype(dtype)
        y = jnp.concatenate([y_r, y_a], axis=-1) @ w_out[l]
        res = DEEPNORM_ALPHA * x.astype(jnp.float32) + gate[:, None, :].astype(jnp.float32) * y.astype(jnp.float32)
        x = (_layer_norm(res) * ln_gain[l].astype(jnp.float32) + ln_bias[l].astype(jnp.float32)).astype(dtype)
    return x
```

```python
import numpy as np
from contextlib import ExitStack

import concourse.bass as bass
import concourse.mybir as mybir
from concourse.bass_utils import run_bass_kernel_spmd

F32 = mybir.dt.float32
BF16 = mybir.dt.bfloat16
AF = mybir.ActivationFunctionType
ALU = mybir.AluOpType
AX = mybir.AxisListType

D = 1024
NCH = 8
TT = 512
NH = 4
DH = 64
NCOL = 1540
C_Q, C_K, C_XR, C_ZR, C_ZA, C_V, C_FL = 0, 256, 512, 768, 1024, 1280, 1536
EPS = 1e-5
NEG = -30000.0
N_CORES = 4


class Sync:
    def __init__(self, nc, es):
        self.nc = nc
        self.es = es
        self.engs = {"pe": nc.tensor, "act": nc.scalar, "dve": nc.vector, "pool": nc.gpsimd, "sp": nc.sync}
        self.sem = {}
        self.cnt = {}
        for k in ("pe", "act", "dve", "pool"):
            self.sem[k] = es.enter_context(nc.semaphore("s_" + k))
            self.cnt[k] = 0
        self.seen = {k: {} for k in self.engs}
        self.lw = {}
        self.rd = {}
        self.n_wait = 0

    def dma_slot(self, name):
        src = "dma:" + name
        if src not in self.sem:
            self.sem[src] = self.es.enter_context(self.nc.semaphore("d_" + name))
            self.cnt[src] = 0
        return src

    def _wait(self, E, src, val):
        if val <= 0 or self.seen[E].get(src, 0) >= val:
            return
        self.engs[E].wait_ge(self.sem[src], val)
        self.seen[E][src] = val
        self.n_wait += 1

    def _deps(self, E, reads, writes):
        deps = {}

        def add(src, val):
            if deps.get(src, 0) < val:
                deps[src] = val

        for k in reads:
            w = self.lw.get(k)
            if w is not None:
                add(*w)
        for k in writes:
            w = self.lw.get(k)
            if w is not None:
                add(*w)
            for src, val in self.rd.get(k, {}).items():
                add(src, val)
        if E == "pe":
            deps.pop("pe", None)
        for src, val in deps.items():
            self._wait(E, src, val)

    @staticmethod
    def _norm(keys):
        out = []
        for k in keys:
            if k == "misc":
                k = ("ps", 6)
            elif k in ("pst", "pss"):
                k = ("ps", 7)
            out.append(k)
        return out

    def op(self, E, fn, reads=(), writes=()):
        reads, writes = self._norm(reads), self._norm(writes)
        writes = list(writes) + [k for k in reads if isinstance(k, tuple) and k[0] == "ps" and k not in writes]
        self._deps(E, reads, writes)
        ins = fn(self.engs[E])
        self.cnt[E] += 1
        ins.then_inc(self.sem[E], 1)
        val = self.cnt[E]
        for k in writes:
            self.lw[k] = (E, val)
            self.rd[k] = {}
        for k in reads:
            self.rd.setdefault(k, {})[E] = val
        return ins

    def dma(self, slot, fn, reads=(), writes=(), q="sp", first=True):
        src = self.dma_slot(slot)
        if first:
            self._wait(q, src, 16 * self.cnt[src])
        self._deps(q, reads, writes)
        ins = fn(self.engs[q])
        self.cnt[src] += 1
        ins.then_inc(self.sem[src], 16)
        val = 16 * self.cnt[src]
        for k in writes:
            self.lw[k] = (src, val)
            self.rd[k] = {}
        for k in reads:
            self.rd.setdefault(k, {})[src] = val

    def barrier(self):
        for E in ("pe", "act", "dve", "pool", "sp"):
            for src in list(self.sem.keys()):
                if src == E:
                    continue
                v = self.cnt[src] * (16 if src.startswith("dma:") else 1)
                self._wait(E, src, v)


def build_program(S, NL, debug=False):
    NT = S // TT
    NB = S // 128
    nc = bass.Bass("TRN2", target_bir_lowering=False)

    def din(name, shape, dt=F32):
        return nc.dram_tensor(name, list(shape), dt, kind="ExternalInput").ap()

    x_in = din("x", [S, D])
    c_col = din("c_col", [128, NCH])
    w_ada = din("w_ada", [NL, D, 3 * D])
    b_ada = din("b_ada", [NL, 1, 3 * D])
    w_in = din("w_in", [NL, 2, D, NCOL])
    b_fg = din("b_fg", [NL, 2, NH, 1])
    chanvec = din("chanvec", [NL, 2, 128, 20])
    w_gate = din("w_gate", [NL, 2, 4, 128, 128])
    w_out = din("w_out", [NL, NCH, 128, D])
    ln_g = din("ln_g", [NL, 1, D])
    ln_b = din("ln_b", [NL, 1, D])
    ident_in = din("ident", [128, 128])
    tri_in = din("tri", [128, 128])
    y_out = nc.dram_tensor("y", [S, D], F32, kind="ExternalOutput").ap()
    x_mid = nc.dram_tensor("x_mid", [S, D], F32).ap()
    yT = nc.dram_tensor("yT", [NCH, 128, S], BF16).ap()
    dbg = {}

    alpha = float((2.0 * 2) ** 0.25)

    with ExitStack() as es:
        sy = Sync(nc, es)
        op = sy.op

        uniq = [0]

        def T(name, shape, dt=F32, stack=es):
            uniq[0] += 1
            return stack.enter_context(nc.sbuf_tensor("%s_%d" % (name, uniq[0]), list(shape), dt))

        PSA = es.enter_context(nc.psum_tensor("psa", [128, 6, 512], F32))
        MISC = es.enter_context(nc.psum_tensor("misc", [128, 512], F32))
        B7 = es.enter_context(nc.psum_tensor("b7", [128, 512], F32))
        PST = B7[:, 0:256].bitcast(BF16).rearrange("p (c t) -> p c t", c=4)
        PSS = B7[:, 256:288]

        def bk(n):
            return ("ps", n)

        ident_f = T("ident_f", [128, 128])
        tri_f = T("tri_f", [128, 128])
        ident = T("ident_b", [128, 128], BF16)
        tri = T("tri_b", [128, 128], BF16)
        ones_f = T("ones_f", [128, 128])
        ones_b = T("ones_b", [128, 1], BF16)
        cact = T("cact", [128, NCH])
        s1_col = T("s1_col", [128, NCH])
        sh_col = T("sh_col", [128, NCH])
        gate_bc = T("gate_bc", [128, D])
        ss = T("ss", [128, 4, NB])

        sy.dma("c0", lambda e: e.dma_start(out=ident_f[:], in_=ident_in[:, :]), writes=["ident_f"])
        sy.dma("c1", lambda e: e.dma_start(out=tri_f[:], in_=tri_in[:, :]), writes=["tri_f"])
        sy.dma("c2", lambda e: e.dma_start(out=cact[:], in_=c_col[:, :]), writes=["cact"])
        op("dve", lambda e: e.tensor_copy(out=ident[:], in_=ident_f[:]), reads=["ident_f"], writes=["ident"])
        op("dve", lambda e: e.tensor_copy(out=tri[:], in_=tri_f[:]), reads=["tri_f"], writes=["tri"])
        op("dve", lambda e: e.memset(ones_f[:], 1.0), writes=["ones_f"])
        op("dve", lambda e: e.memset(ones_b[:], 1.0), writes=["ones_b"])
        with ExitStack() as ph:
            ce = T("ce", [128, NCH], stack=ph)
            op("act", lambda e: e.activation(out=ce[:], in_=cact[:], func=AF.Exp, scale=-1.0), reads=["cact"], writes=["ce"])
            op("dve", lambda e: e.tensor_scalar_add(out=ce[:], in0=ce[:], scalar1=1.0), reads=["ce"], writes=["ce"])
            op("dve", lambda e: e.reciprocal(out=ce[:], in_=ce[:]), reads=["ce"], writes=["ce"])
            op("dve", lambda e: e.tensor_mul(out=cact[:], in0=cact[:], in1=ce[:]), reads=["ce", "cact"], writes=["cact"])
            sy.barrier()

        for L in range(NL):
            x_src = x_in if L == 0 else x_mid
            x_dst = y_out if L == NL - 1 else x_mid
            xs_key = "x_in" if L == 0 else "x_mid"
            xd_key = "y_out" if L == NL - 1 else "x_mid"

            with ExitStack() as ph:
                wa = [T("wa%d" % i, [128, 3 * D], stack=ph) for i in range(2)]
                bst = [T("bst%d" % i, [128, 512], stack=ph) for i in range(2)]
                s1_bc = T("s1_bc", [128, D], stack=ph)
                sh_bc = T("sh_bc", [128, D], stack=ph)
                dtmp = T("dtmp", [128, 128], stack=ph)
                for c in range(NCH):
                    buf = wa[c % 2]
                    key = "wa%d" % (c % 2)
                    sy.dma(key, lambda e: e.dma_start(out=buf[:], in_=w_ada[L, c * 128:(c + 1) * 128, :]), writes=[key])
                    for nb in range(6):
                        op("pe", lambda e: e.matmul(PSA[:, nb, :], lhsT=cact[:, c:c + 1].to_broadcast([128, 128]), rhs=buf[:, nb * 512:(nb + 1) * 512],
                                                    start=(c == 0), stop=(c == NCH - 1)),
                           reads=[key, "cact"], writes=[bk(nb)])
                for nb in range(6):
                    vi, hb = nb // 2, nb % 2
                    dst, dkey, addc = ((sh_bc, "sh_bc", 0.0), (s1_bc, "s1_bc", 1.0), (gate_bc, "gate_bc", 0.0))[vi]
                    bb = bst[nb % 2]
                    bkey = "bst%d" % (nb % 2)
                    sy.dma(bkey, lambda e: e.dma_start(out=bb[:], in_=b_ada[L, :, nb * 512:(nb + 1) * 512].partition_broadcast(128)), writes=[bkey])
                    op("dve", lambda e: e.scalar_tensor_tensor(out=dst[:, hb * 512:(hb + 1) * 512], in0=PSA[:, nb, :], scalar=addc, in1=bb[:],
                                                               op0=ALU.add, op1=ALU.add),
                       reads=[bk(nb), bkey], writes=[dkey])
                for src, skey, dstc, dck in ((s1_bc, "s1_bc", s1_col, "s1_col"), (sh_bc, "sh_bc", sh_col, "sh_col")):
                    for c in range(NCH):
                        op("dve", lambda e: e.tensor_mul(out=dtmp[:], in0=src[:, c * 128:(c + 1) * 128], in1=ident_f[:]),
                           reads=[skey, "ident_f"], writes=["dtmp"])
                        op("dve", lambda e: e.reduce_sum(out=dstc[:, c:c + 1], in_=dtmp[:], axis=AX.X), reads=["dtmp"], writes=[dck])
                sy.barrier()

            for g in range(2):
                with ExitStack() as ph:
                    Wp = T("Wp", [128, NCH, NCOL], BF16, stack=ph)
                    Kc = [T("Kc%d" % h, [128, S], BF16, stack=ph) for h in range(NH)]
                    Vc = T("Vc", [128, NB, NH, DH + 1], BF16, stack=ph)
                    Qa = [[T("Qa%d_%d" % (p_, h), [128, TT], BF16, stack=ph) for h in range(NH)] for p_ in range(2)]
                    Wg = T("Wg", [128, 4, 128], BF16, stack=ph)
                    cv = T("cv", [128, 20], stack=ph)
                    bcol = T("bcol", [128, 13], stack=ph)
                    nbcol = T("nbcol", [128, 12], stack=ph)
                    bv_bc = T("bv_bc", [128, 256], stack=ph)
                    cvd = T("cvd", [128, 8], stack=ph)
                    nbf = T("nbf", [NH, 1], stack=ph)

                    with ExitStack() as ph2:
                        wst = [T("wst%d" % i, [128, NCOL], stack=ph2) for i in range(2)]
                        wgs = T("wgs", [128, 4, 128], stack=ph2)
                        brow_s = T("brow_s", [1, NCOL], stack=ph2)
                        for c in range(NCH):
                            buf = wst[c % 2]
                            key = "wst%d" % (c % 2)
                            sy.dma(key, lambda e: e.dma_start(out=buf[:], in_=w_in[L, g, c * 128:(c + 1) * 128, :]), writes=[key])
                            for nb4 in range(4):
                                c0, c1 = nb4 * 512, min(NCOL, (nb4 + 1) * 512)
                                op("pe", lambda e: e.matmul(PSA[0:1, nb4, 0:c1 - c0], lhsT=sh_col[:, c:c + 1], rhs=buf[:, c0:c1],
                                                            start=(c == 0), stop=(c == NCH - 1)),
                                   reads=[key, "sh_col"], writes=[bk(nb4)])
                            if c % 2 == 0:
                                op("dve", lambda e: e.tensor_scalar(out=Wp[:, c, :], in0=buf[:], scalar1=s1_col[:, c:c + 1], scalar2=None, op0=ALU.mult),
                                   reads=[key, "s1_col"], writes=[("Wp", c)])
                            else:
                                op("pool", lambda e: e.tensor_scalar(out=Wp[:, c, :], in0=buf[:], scalar1=s1_col[:, c:c + 1], scalar2=0.0,
                                                                     op0=ALU.mult, op1=ALU.add),
                                   reads=[key, "s1_col"], writes=[("Wp", c)])
                        for nb4 in range(4):
                            c0, c1 = nb4 * 512, min(NCOL, (nb4 + 1) * 512)
                            op("dve", lambda e: e.tensor_copy(out=brow_s[0:1, c0:c1], in_=PSA[0:1, nb4, 0:c1 - c0]), reads=[bk(nb4)], writes=["brow_s"])
                        for m in range(13):
                            mm = 128 if m < 12 else NH
                            op("pe", lambda e: e.matmul(MISC[0:mm, m:m + 1], lhsT=brow_s[0:1, m * 128:m * 128 + mm], rhs=ones_f[0:1, 0:1],
                                                        start=True, stop=True), reads=["brow_s", "ones_f"], writes=["misc"])
                        op("dve", lambda e: e.tensor_copy(out=bcol[:, 0:12], in_=MISC[:, 0:12]), reads=["misc"], writes=["bcol"])
                        op("dve", lambda e: e.tensor_copy(out=bcol[0:NH, 12:13], in_=MISC[0:NH, 12:13]), reads=["misc"], writes=["bcol"])
                        op("dve", lambda e: e.tensor_scalar(out=nbcol[:, 0:12], in0=MISC[:, 0:12], scalar1=-1.0, scalar2=None, op0=ALU.mult),
                           reads=["misc"], writes=["nbcol"])
                        op("pe", lambda e: e.matmul(PSA[:, 4, 0:256], lhsT=ones_f[0:1, :], rhs=brow_s[0:1, C_V:C_V + 256], start=True, stop=True),
                           reads=["brow_s", "ones_f"], writes=[bk(4)])
                        op("dve", lambda e: e.tensor_copy(out=bv_bc[:], in_=PSA[:, 4, 0:256]), reads=[bk(4)], writes=["bv_bc"])
                        sy.dma("wgs", lambda e: e.dma_start(out=wgs[:], in_=w_gate[L, g].rearrange("k p m -> p k m")), writes=["wgs"])
                        op("pool", lambda e: e.tensor_copy(out=Wg[:], in_=wgs[:]), reads=["wgs"], writes=["Wg"])
                        sy.barrier()
                    xb = [T("xb%d" % i, [128, D], stack=ph) for i in range(2)]
                    hb16 = T("hb16", [128, D], BF16, stack=ph)
                    hT = T("hT", [128, NCH, TT], BF16, stack=ph)
                    st6 = T("st6", [128, 4, 2, 6], stack=ph)
                    mv = T("mv", [128, 4, 2], stack=ph)
                    rstd = T("rstd", [128, 4], stack=ph)
                    xr = [T("xr%d" % i, [128, TT + 3], stack=ph) for i in range(2)]
                    hcar = T("hcar", [128, 2], stack=ph)
                    Gr = T("Gr", [128, 2, TT], stack=ph)
                    Ga = [T("Ga%d" % p_, [128, 2, TT], stack=ph) for p_ in range(2)]
                    ta = T("ta", [128, TT], stack=ph)
                    tb_ = T("tb", [128, TT], stack=ph)
                    tcc = T("tc", [128, TT], stack=ph)
                    td = T("td", [128, TT], stack=ph)
                    te = T("te", [128, TT], stack=ph)
                    tf = T("tf", [128, TT], stack=ph)
                    xcb = T("xcb", [128, TT], BF16, stack=ph)
                    sqb = T("sqb", [128, TT], BF16, stack=ph)
                    ysl = [T("ysl%d" % p_, [128, 2, TT], BF16, stack=ph) for p_ in range(2)]
                    ysa = T("ysa", [128, 2, TT], BF16, stack=ph)
                    fin1 = T("fin1", [128, TT], stack=ph)
                    fin2 = T("fin2", [128, TT], stack=ph)
                    PT = [T("PT%d" % i, [128, 2 * TT], BF16, stack=ph) for i in range(3)]
                    sz = T("sz", [128, TT], stack=ph)
                    se = T("se", [128, TT], stack=ph)
                    flv = Kc[0][96:96 + NH, :].bitcast(F32)
                    fle, nd, r1, onesr = flv[:, 0:TT], flv[:, TT:2 * TT], flv[:, 2 * TT:3 * TT], flv[:, 3 * TT:4 * TT]
                    ndc = Kc[1][96:96 + NH, 3 * TT:3 * TT + 2].bitcast(F32)
                    spl = Kc[1][96:96 + NH, 0:3 * TT].rearrange("p (a t) -> p a t", a=3)
                    sqa = T("sqa", [128, TT], BF16, stack=ph)
                    pss_sb = T("pss_sb", [128, 24], stack=ph)

                    sy.dma("cv", lambda e: e.dma_start(out=cv[:], in_=chanvec[L, g, :, :]), writes=["cv"])
                    sy.dma("nbf", lambda e: e.dma_start(out=nbf[:], in_=b_fg[L, g, :, :]), writes=["nbf"])
                    op("dve", lambda e: e.tensor_add(out=nbf[:], in0=nbf[:], in1=bcol[0:NH, 12:13]), reads=["nbf", "bcol"], writes=["nbf"])
                    op("dve", lambda e: e.tensor_scalar(out=nbf[:], in0=nbf[:], scalar1=-1.0, scalar2=None, op0=ALU.mult), reads=["nbf"], writes=["nbf"])
                    for cg in range(2):
                        o = cg * 9
                        op("dve", lambda e: e.tensor_scalar_mul(out=cvd[:, cg * 4:cg * 4 + 2], in0=cv[:, o + 5:o + 7], scalar1=-1.0),
                           reads=["cv"], writes=["cvd"])
                        op("act", lambda e: e.activation(out=cvd[:, cg * 4 + 2:cg * 4 + 3], in_=cv[:, o + 7:o + 8], func=AF.Exp, scale=-1.0),
                           reads=["cv", "cvd"], writes=["cvd"])
                        op("act", lambda e: e.activation(out=cvd[:, cg * 4 + 2:cg * 4 + 3], in_=cvd[:, cg * 4 + 2:cg * 4 + 3], func=AF.Ln, bias=1.0),
                           reads=["cvd"], writes=["cvd"])
                        op("dve", lambda e: e.tensor_scalar_mul(out=cvd[:, cg * 4 + 3:cg * 4 + 4], in0=cvd[:, cg * 4 + 2:cg * 4 + 3], scalar1=-16.0),
                           reads=["cvd"], writes=["cvd"])
                        op("dve", lambda e: e.tensor_scalar_mul(out=cvd[:, cg * 4 + 2:cg * 4 + 3], in0=cvd[:, cg * 4 + 2:cg * 4 + 3], scalar1=-8.0),
                           reads=["cvd"], writes=["cvd"])
                        op("pool", lambda e: e.memset(xr[cg][:, 0:3], 0.0), writes=[("xr", cg)])
                    op("pool", lambda e: e.memset(hcar[:], 0.0), writes=["hcar"])
                    op("pool", lambda e: e.memset(ndc, 0.0), writes=["ndc"])
                    op("pool", lambda e: e.memset(onesr, 1.0), writes=["onesr"])
                    op("pool", lambda e: e.memset(Vc[:, :, :, DH:DH + 1], 1.0), writes=["Vones"])
                    for h in range(NH):
                        op("pool", lambda e: e.memset(Kc[h][64:70, :], -1.0), writes=[("Kones", h)])
                        for p_ in range(2):
                            op("pool", lambda e: e.memset(Qa[p_][h][64:70, :], 1.0), writes=[("Qa", p_, h)])

                    prot = [0]
                    pcur = [None, None]

                    def pbank(i):
                        if i <= 15:
                            prot[0] += 1
                            n = (6, 1, 3)[prot[0] % 3]
                        else:
                            n = 6
                        ap = MISC if n == 6 else PSA[:, n, :]
                        pcur[0], pcur[1] = ap, bk(n)
                        return ap, bk(n)

                    def proj(i, col0, ncols, M=128):
                        ap, key = pbank(i)
                        for c in range(NCH):
                            op("pe", lambda e: e.matmul(ap[0:M, :], lhsT=Wp[:, c, col0:col0 + ncols], rhs=hT[:, c, :],
                                                        start=(c == 0), stop=(c == NCH - 1)),
                               reads=[("Wp", c), "hT"], writes=[key])
                        return ap, key

                    def pre(i):
                        par = i % 2
                        def ln_a(j):
                            blk = i * 4 + j
                            xt = xb[blk % 2]
                            xk = "xb%d" % (blk % 2)
                            sy.dma(xk, lambda e: e.dma_start(out=xt[:], in_=x_src[blk * 128:(blk + 1) * 128, :]),
                                   reads=[(xs_key, blk)], writes=[xk])
                            for hh in range(2):
                                op("dve", lambda e: e.bn_stats(out=st6[:, j, hh, :], in_=xt[:, hh * 512:(hh + 1) * 512]),
                                   reads=[xk], writes=[("st6", j, hh)])
                            op("dve", lambda e: e.bn_aggr(out=mv[:, j, :], in_=st6[:, j, :, :]), reads=[("st6", j, 0), ("st6", j, 1)], writes=[("mv", j)])
                            op("act", lambda e: e.activation(out=rstd[:, j:j + 1], in_=mv[:, j, 1:2], func=AF.Ln, bias=EPS), reads=[("mv", j)], writes=[("rstd", j)])
                            op("act", lambda e: e.activation(out=rstd[:, j:j + 1], in_=rstd[:, j:j + 1], func=AF.Exp, scale=-0.5), reads=[("rstd", j)], writes=[("rstd", j)])

                        def ln_b(j):
                            blk = i * 4 + j
                            xt = xb[blk % 2]
                            xk = "xb%d" % (blk % 2)
                            op("dve", lambda e: e.tensor_scalar(out=hb16[:], in0=xt[:], scalar1=mv[:, j, 0:1], scalar2=rstd[:, j:j + 1],
                                                                op0=ALU.subtract, op1=ALU.mult),
                               reads=[xk, ("mv", j), ("rstd", j)], writes=["hb16"])
                            if i <= 8:
                                n = 1 if j % 2 == 0 else 3
                                tp8 = PSA[:, n, :].bitcast(BF16).rearrange("p (c t) -> p c t", c=8)
                                for c in range(NCH):
                                    op("pe", lambda e: e.transpose(out=tp8[:, c, :], in_=hb16[:, c * 128:(c + 1) * 128], identity=ident[:]),
                                       reads=["hb16", "ident"], writes=[bk(n)])
                                op("dve", lambda e: e.tensor_copy(out=hT[:, :, j * 128:(j + 1) * 128], in_=tp8), reads=[bk(n)], writes=["hT"])
                            else:
                                for half in range(2):
                                    for cc in range(4):
                                        c = half * 4 + cc
                                        op("pe", lambda e: e.transpose(out=PST[:, cc, :], in_=hb16[:, c * 128:(c + 1) * 128], identity=ident[:]),
                                           reads=["hb16", "ident"], writes=["pst"])
                                    op("dve", lambda e: e.tensor_copy(out=hT[:, half * 4:half * 4 + 4, j * 128:(j + 1) * 128], in_=PST),
                                       reads=["pst"], writes=["hT"])
                        ln_a(0)
                        yield
                        ln_a(1)
                        yield
                        for j in range(4):
                            ln_b(j)
                            yield
                            if j + 2 < 4:
                                ln_a(j + 2)
                                yield
                        for cg in range(2):
                            pa, pk = proj(i, C_XR + cg * 128, 128)
                            op("dve", lambda e: e.tensor_scalar(out=xr[cg][:, 3:TT + 3], in0=pa[:, :], scalar1=bcol[:, 4 + cg:5 + cg], scalar2=None, op0=ALU.add),
                               reads=[pk, "bcol"], writes=[("xr", cg)])
                            yield
                        gr_ready = [False, False]

                        def chain_b():
                            def silu(kind, cg):
                                pa, pk = proj(i, (C_ZR if kind == 0 else C_ZA) + cg * 128, 128)
                                Gt = Gr if kind == 0 else Ga[par]
                                gk = ("Gr", cg) if kind == 0 else ("Ga", par, cg)
                                gcol = (cg * 9 + 8) if kind == 0 else (18 + cg)
                                mcol = (6 if kind == 0 else 8) + cg
                                op("dve", lambda e: e.tensor_scalar(out=sz[:], in0=pa[:, :], scalar1=bcol[:, mcol:mcol + 1], scalar2=None, op0=ALU.add),
                                   reads=[pk, "bcol"], writes=["sz"])
                                yield
                                op("act", lambda e: e.activation(out=se[:], in_=sz[:], func=AF.Exp, scale=-1.0), reads=["sz"], writes=["se"])
                                op("act", lambda e: e.activation(out=se[:], in_=se[:], func=AF.Ln, bias=1.0), reads=["se"], writes=["se"])
                                op("act", lambda e: e.activation(out=se[:], in_=se[:], func=AF.Exp, scale=-1.0), reads=["se"], writes=["se"])
                                op("pool", lambda e: e.tensor_scalar(out=sz[:], in0=sz[:], scalar1=cv[:, gcol:gcol + 1], scalar2=0.0, op0=ALU.mult, op1=ALU.add),
                                   reads=["sz", "cv"], writes=["sz"])
                                yield
                                op("pool", lambda e: e.tensor_mul(out=Gt[:, cg, :], in0=sz[:], in1=se[:]), reads=["se", "sz"], writes=[gk])
                                if kind == 0:
                                    gr_ready[cg] = True
                                yield
                            for cg in range(2):
                                yield from silu(0, cg)
                            for hp in range(2):
                                pa, pk = proj(i, C_Q + hp * 128, 128)
                                for s_ in range(2):
                                    h = hp * 2 + s_
                                    op("dve", lambda e: e.tensor_scalar(out=Qa[par][h][0:64, :], in0=pa[s_ * 64:(s_ + 1) * 64, :],
                                                                        scalar1=bcol[s_ * 64:(s_ + 1) * 64, hp:hp + 1], scalar2=0.125, op0=ALU.add, op1=ALU.mult),
                                       reads=[pk, "bcol"], writes=[("Qa", par, h)])
                                yield
                            for hp in range(2):
                                pa, pk = proj(i, C_K + hp * 128, 128)
                                for s_ in range(2):
                                    h = hp * 2 + s_
                                    op("dve", lambda e: e.tensor_scalar(out=Kc[h][0:64, i * TT:(i + 1) * TT], in0=pa[s_ * 64:(s_ + 1) * 64, :],
                                                                        scalar1=bcol[s_ * 64:(s_ + 1) * 64, 2 + hp:3 + hp], scalar2=None, op0=ALU.add),
                                       reads=[pk, "bcol"], writes=[("K", h, i)])
                                yield
                            pa, pk = pbank(i)
                            for c in range(NCH):
                                op("pe", lambda e: e.matmul(pa[0:NH, :], lhsT=Wp[:, c, C_FL:C_FL + NH], rhs=hT[:, c, :],
                                                            start=(c == 0), stop=(c == NCH - 1)),
                                   reads=[("Wp", c), "hT"], writes=[pk])
                            op("act", lambda e: e.activation(out=fle, in_=pa[0:NH, :], func=AF.Exp, scale=-1.0, bias=nbf[:, 0:1]),
                               reads=[pk, "nbf"], writes=["fle"])
                            yield
                            op("act", lambda e: e.activation(out=fle, in_=fle, func=AF.Ln, bias=1.0), reads=["fle"], writes=["fle"])
                            yield
                            op("dve", lambda e: e.tensor_tensor_scan(out=nd, data0=onesr, data1=fle, initial=ndc,
                                                                     op0=ALU.mult, op1=ALU.add),
                               reads=["fle", "ndc", "onesr"], writes=["nd"])
                            op("dve", lambda e: e.tensor_copy(out=ndc, in_=nd[:, TT - 1:TT]), reads=["nd"], writes=["ndc"])
                            yield
                            op("dve", lambda e: e.tensor_copy(out=spl[:, 0, :], in_=nd), reads=["nd"], writes=["spl"])
                            op("dve", lambda e: e.tensor_sub(out=r1, in0=nd, in1=spl[:, 0, :]), reads=["nd", "spl"], writes=["r1"])
                            yield
                            op("dve", lambda e: e.tensor_copy(out=spl[:, 1, :], in_=r1), reads=["r1"], writes=["spl"])
                            op("dve", lambda e: e.tensor_sub(out=r1, in0=r1, in1=spl[:, 1, :]), reads=["r1", "spl"], writes=["r1"])
                            yield
                            op("dve", lambda e: e.tensor_copy(out=spl[:, 2, :], in_=r1), reads=["r1"], writes=["spl"])
                            yield
                            for h in range(NH):
                                sy.dma("augK%d" % h, lambda e: e.dma_start(out=Kc[h][64:67, i * TT:(i + 1) * TT], in_=spl[h:h + 1, :, :]),
                                       reads=["spl", ("Kones", h)], writes=[("Kaug", h, i)])
                                sy.dma("augQ%d" % h, lambda e: e.dma_start(out=Qa[par][h][67:70, :], in_=spl[h:h + 1, :, :]),
                                       reads=["spl"], writes=[("Qa", par, h)])
                            yield
                            for cg in range(2):
                                yield from silu(1, cg)
                            for jb in range(4):
                                pa, pk = pbank(i)
                                for c in range(NCH):
                                    op("pe", lambda e: e.matmul(pa[:, 0:256], lhsT=hT[:, c, jb * 128:(jb + 1) * 128], rhs=Wp[:, c, C_V:C_V + 256],
                                                                start=(c == 0), stop=(c == NCH - 1)),
                                       reads=[("Wp", c), "hT"], writes=[pk])
                                op("dve", lambda e: e.tensor_tensor(out=Vc[:, i * 4 + jb, :, 0:DH],
                                                                    in0=pa[:, 0:256].rearrange("p (h d) -> p h d", h=NH),
                                                                    in1=bv_bc[:].rearrange("p (h d) -> p h d", h=NH), op=ALU.add),
                                   reads=[pk, "bv_bc"], writes=[("V", i)])
                                yield

                        def chain_a():
                            for cg in range(2):
                                o = cg * 9
                                xk = ("xr", cg)
                                X = xr[cg]
                                op("dve", lambda e: e.tensor_scalar(out=ta[:], in0=X[:, 0:TT], scalar1=cv[:, o:o + 1], scalar2=cv[:, o + 4:o + 5],
                                                                    op0=ALU.mult, op1=ALU.add), reads=[xk, "cv"], writes=["ta"])
                                for k in range(1, 4):
                                    op("dve", lambda e: e.scalar_tensor_tensor(out=ta[:], in0=X[:, k:k + TT], scalar=cv[:, o + k:o + k + 1], in1=ta[:],
                                                                               op0=ALU.mult, op1=ALU.add), reads=[xk, "cv", "ta"], writes=["ta"])
                                yield
                                op("pool", lambda e: e.tensor_copy(out=X[:, 0:3], in_=X[:, TT:TT + 3]), reads=[xk], writes=[xk])
                                op("dve", lambda e: e.tensor_copy(out=xcb[:], in_=ta[:]), reads=["ta"], writes=["xcb"])
                                yield
                                op("pe", lambda e: e.matmul(MISC[:, :], lhsT=Wg[:, cg * 2, :], rhs=xcb[:], start=True, stop=True),
                                   reads=["Wg", "xcb"], writes=["misc"])
                                op("act", lambda e: e.activation(out=tb_[:], in_=MISC[:, :], func=AF.Exp, scale=-1.0, bias=cvd[:, cg * 4:cg * 4 + 1]),
                                   reads=["misc", "cvd"], writes=["tb"])
                                yield
                                op("pe", lambda e: e.matmul(MISC[:, :], lhsT=Wg[:, cg * 2 + 1, :], rhs=xcb[:], start=True, stop=True),
                                   reads=["Wg", "xcb"], writes=["misc"])
                                op("act", lambda e: e.activation(out=tcc[:], in_=MISC[:, :], func=AF.Exp, scale=-1.0, bias=cvd[:, cg * 4 + 1:cg * 4 + 2]),
                                   reads=["misc", "cvd"], writes=["tc"])
                                op("act", lambda e: e.activation(out=tb_[:], in_=tb_[:], func=AF.Ln, bias=1.0), reads=["tb"], writes=["tb"])
                                yield
                                op("act", lambda e: e.activation(out=tcc[:], in_=tcc[:], func=AF.Ln, bias=1.0), reads=["tc"], writes=["tc"])
                                op("act", lambda e: e.activation(out=tb_[:], in_=tb_[:], func=AF.Exp, scale=-1.0), reads=["tb"], writes=["tb"])
                                yield
                                op("act", lambda e: e.activation(out=tcc[:], in_=tcc[:], func=AF.Exp, scale=-1.0), reads=["tc"], writes=["tc"])
                                op("act", lambda e: e.activation(out=td[:], in_=tb_[:], func=AF.Exp, scale=cvd[:, cg * 4 + 2:cg * 4 + 3]),
                                   reads=["tb", "cvd"], writes=["td"])
                                yield
                                op("act", lambda e: e.activation(out=te[:], in_=tb_[:], func=AF.Exp, scale=cvd[:, cg * 4 + 3:cg * 4 + 4]),
                                   reads=["tb", "cvd"], writes=["te"])
                                op("dve", lambda e: e.tensor_mul(out=tcc[:], in0=tcc[:], in1=ta[:]), reads=["tc", "ta"], writes=["tc"])
                                yield
                                op("act", lambda e: e.activation(out=te[:], in_=te[:], func=AF.Ln, scale=-1.0, bias=1.0), reads=["te"], writes=["te"])
                                op("act", lambda e: e.activation(out=te[:], in_=te[:], func=AF.Exp, scale=0.5), reads=["te"], writes=["te"])
                                yield
                                op("dve", lambda e: e.tensor_mul(out=tcc[:], in0=tcc[:], in1=te[:]), reads=["tc", "te"], writes=["tc"])
                                yield
                                op("dve", lambda e: e.tensor_tensor_scan(out=tf[:], data0=td[:], data1=tcc[:], initial=hcar[:, cg:cg + 1],
                                                                         op0=ALU.mult, op1=ALU.add),
                                   reads=["td", "tc", "hcar"], writes=["tf"])
                                op("dve", lambda e: e.tensor_copy(out=hcar[:, cg:cg + 1], in_=tf[:, TT - 1:TT]), reads=["tf"], writes=["hcar"])
                                yield
                                while not gr_ready[cg]:
                                    yield "blocked"
                                op("pool", lambda e: e.tensor_mul(out=ysl[par][:, cg, :], in0=tf[:], in1=Gr[:, cg, :]), reads=["tf", ("Gr", cg)], writes=[("ysl", par, cg)])
                                op("pool", lambda e: e.tensor_mul(out=sqb[:], in0=tf[:], in1=tf[:]), reads=["tf"], writes=["sqb"])
                                yield
                                for jb in range(4):
                                    col = cg * 4 + jb
                                    op("pe", lambda e: e.matmul(PSS[:, col:col + 1], lhsT=sqb[:, jb * 128:(jb + 1) * 128], rhs=ones_b[:, 0:1],
                                                                start=True, stop=True), reads=["sqb", "ones_b"], writes=["pss"])
                                yield
                            op("dve", lambda e: e.tensor_copy(out=pss_sb[:, 0:8], in_=PSS[:, 0:8]), reads=["pss"], writes=["pss_l"])
                            op("pool", lambda e: e.tensor_add(out=ss[:, g * 2, i * 4:(i + 1) * 4], in0=pss_sb[:, 0:4], in1=pss_sb[:, 4:8]),
                               reads=["pss_l"], writes=[("ss", g, 0)])
                            sy.dma("ysl%d" % par, lambda e: e.dma_start(out=yT[g * 4:g * 4 + 2, :, i * TT:(i + 1) * TT].rearrange("c p t -> p c t"), in_=ysl[par][:]),
                                   reads=[("ysl", par, 0), ("ysl", par, 1)], writes=[("yTl", g, i)])
                            yield

                        A, B = chain_a(), chain_b()
                        doneA = doneB = False
                        END = object()
                        while not (doneA and doneB):
                            if not doneB:
                                if next(B, END) is END:
                                    doneB = True
                                else:
                                    yield
                            if not doneA:
                                r = next(A, END)
                                if r is END:
                                    doneA = True
                                elif r != "blocked":
                                    yield

                    def attention(i, pump):
                        par = i % 2
                        steps = []
                        for h in range(NH):
                            hs = []
                            for p in range(2 * i):
                                hs.append(("pair", h, 2 * p))
                            for j in range(4):
                                hs.append(("diag", h, j))
                            for n_, s_ in enumerate(hs):
                                steps.append(s_ + (n_ == 0, n_ == len(hs) - 1))

                        deferred = []

                        def run_deferred(force=False):
                            k = 0
                            while k < len(deferred):
                                deferred[k][0] -= 1
                                if deferred[k][0] <= 0 or force:
                                    deferred.pop(k)[1]()
                                else:
                                    k += 1

                        def emit_qk(si, st):
                            kind, h, a, first, last = st
                            sb = si % 2
                            if kind == "pair":
                                for u in range(2):
                                    kt = a + u
                                    op("pe", lambda e: e.matmul(PSA[:, sb * 2 + u, :], lhsT=Kc[h][0:70, kt * 128:(kt + 1) * 128], rhs=Qa[par][h][0:70, :],
                                                                start=True, stop=True),
                                       reads=[("K", h, kt // 4), ("Kaug", h, kt // 4), ("Kones", h), ("Qa", par, h)], writes=[bk(sb * 2 + u)])
                            else:
                                j = a
                                kt = i * 4 + j
                                op("pe", lambda e: e.matmul(PSA[:, sb * 2, j * 128:TT], lhsT=Kc[h][0:70, kt * 128:(kt + 1) * 128], rhs=Qa[par][h][0:70, j * 128:TT],
                                                            start=True, stop=False),
                                   reads=[("K", h, i), ("Kaug", h, i), ("Kones", h), ("Qa", par, h)], writes=[bk(sb * 2)])
                                op("pe", lambda e: e.matmul(PSA[:, sb * 2, j * 128:(j + 1) * 128], lhsT=ident[:], rhs=tri[:], start=False, stop=True),
                                   reads=["ident", "tri"], writes=[bk(sb * 2)])

                        def emit_exp(si, st):
                            kind, h, a, first, last = st
                            sb = si % 2
                            pb = si % 3
                            if kind == "pair":
                                op("act", lambda e: e.activation(out=PT[pb][:, :], in_=PSA[:, sb * 2:sb * 2 + 2, :].rearrange("p a b -> p (a b)"), func=AF.Exp),
                                   reads=[bk(sb * 2), bk(sb * 2 + 1)], writes=[("PT", pb)])
                            else:
                                j = a
                                op("act", lambda e: e.activation(out=PT[pb][:, j * 128:TT], in_=PSA[:, sb * 2, j * 128:TT], func=AF.Exp),
                                   reads=[bk(sb * 2)], writes=[("PT", pb)])

                        def emit_pv(si, st):
                            kind, h, a, first, last = st
                            pb = si % 3
                            ob = 4 + (h % 2)
                            if kind == "pair":
                                for u in range(2):
                                    kt = a + u
                                    op("pe", lambda e: e.matmul(PSA[0:DH + 1, ob, :], lhsT=Vc[:, kt, h, :], rhs=PT[pb][:, u * TT:(u + 1) * TT],
                                                                start=(first and u == 0), stop=False),
                                       reads=[("V", kt // 4), "Vones", ("PT", pb)], writes=[bk(ob)])
                            else:
                                j = a
                                kt = i * 4 + j
                                op("pe", lambda e: e.matmul(PSA[0:DH + 1, ob, j * 128:TT], lhsT=Vc[:, kt, h, :], rhs=PT[pb][:, j * 128:TT],
                                                            start=first, stop=(j == 3)),
                                   reads=[("V", i), "Vones", ("PT", pb)], writes=[bk(ob)])
                            if last:
                                finalize(h)

                        def finalize(h):
                            ob = 4 + (h % 2)
                            hp, s_ = h // 2, h % 2
                            po = s_ * 64

                            def st1():
                                op("act", lambda e: e.activation(out=fin1[64:65, :], in_=PSA[64:65, ob, :], func=AF.Ln), reads=[bk(ob)], writes=["rinv"])
                                op("act", lambda e: e.activation(out=fin1[64:65, :], in_=fin1[64:65, :], func=AF.Exp, scale=-1.0), reads=["rinv"], writes=["rinv"])
                                deferred.append([2, st2])

                            def st2():
                                op("pe", lambda e: e.matmul(MISC[0:64, :], lhsT=ones_f[64:65, 0:64], rhs=fin1[64:65, :], start=True, stop=True),
                                   reads=["rinv", "ones_f"], writes=["misc"])
                                op("dve", lambda e: e.tensor_copy(out=fin1[0:64, :], in_=MISC[0:64, :]), reads=["misc"], writes=["bcs"])
                                deferred.append([1, st2b])

                            def st2b():
                                op("dve", lambda e: e.tensor_mul(out=fin2[po:po + 64, :], in0=PSA[0:64, ob, :], in1=fin1[0:64, :]), reads=[bk(ob), "bcs"], writes=["yn"])
                                op("pool" if s_ == 0 else "dve", lambda e: e.tensor_mul(out=sqa[0:64, :], in0=fin2[po:po + 64, :], in1=fin2[po:po + 64, :]),
                                   reads=["yn"], writes=["sqa"])
                                op("pool", lambda e: e.tensor_mul(out=ysa[po:po + 64, hp, :], in0=fin2[po:po + 64, :], in1=Ga[par][po:po + 64, hp, :]),
                                   reads=["yn", ("Ga", par, hp)], writes=[("ysa", hp, s_)])
                                deferred.append([2, st3])

                            def st3():
                                for jb in range(4):
                                    col = 8 + h * 4 + jb
                                    op("pe", lambda e: e.matmul(PSS[:, col:col + 1], lhsT=sqa[0:64, jb * 128:(jb + 1) * 128], rhs=ones_b[0:64, 0:1],
                                                                start=True, stop=True), reads=["sqa", "ones_b"], writes=["pss"])
                            deferred.append([1, st1])

                        for si, st in enumerate(steps):
                            emit_qk(si, st)
                            emit_exp(si, st)
                            if si > 1:
                                emit_pv(si - 2, steps[si - 2])
                            run_deferred()
                            pump(len(steps) - si)
                        emit_pv(len(steps) - 2, steps[-2])
                        emit_pv(len(steps) - 1, steps[-1])
                        while deferred:
                            run_deferred(force=True)
                        op("dve", lambda e: e.tensor_copy(out=pss_sb[:, 8:24], in_=PSS[:, 8:24]), reads=["pss"], writes=["pss_a"])
                        op("pool", lambda e: e.tensor_add(out=pss_sb[:, 8:12], in0=pss_sb[:, 8:12], in1=pss_sb[:, 12:16]), reads=["pss_a"], writes=["pss_a"])
                        op("pool", lambda e: e.tensor_add(out=pss_sb[:, 16:20], in0=pss_sb[:, 16:20], in1=pss_sb[:, 20:24]), reads=["pss_a"], writes=["pss_a"])
                        op("pool", lambda e: e.tensor_add(out=ss[:, g * 2 + 1, i * 4:(i + 1) * 4], in0=pss_sb[:, 8:12], in1=pss_sb[:, 16:20]),
                           reads=["pss_a"], writes=[("ss", g, 1)])
                        sy.dma("ysa", lambda e: e.dma_start(out=yT[g * 4 + 2:g * 4 + 4, :, i * TT:(i + 1) * TT].rearrange("c p t -> p c t"), in_=ysa[:]),
                               reads=[("ysa", 0, 0), ("ysa", 0, 1), ("ysa", 1, 0), ("ysa", 1, 1)], writes=[("yTa", g, i)])

                    n_yield = [0]
                    g0 = pre(0)
                    for _ in g0:
                        n_yield[0] += 1
                    for i in range(NT):
                        nxt = pre(i + 1) if i + 1 < NT else None
                        left = [n_yield[0]]

                        n_steps_i = NH * (2 * i + 4)
                        stride = max(1, (n_steps_i - 4) // max(1, n_yield[0]))

                        def pump(steps_left):
                            if nxt is None or left[0] <= 0:
                                return
                            if stride > 1 and (steps_left % stride) != 0:
                                return
                            k = -(-left[0] // max(1, (steps_left - 2) // stride))
                            for _ in range(k):
                                if left[0] <= 0:
                                    break
                                left[0] -= 1
                                try:
                                    next(nxt)
                                except StopIteration:
                                    left[0] = 0
                        attention(i, pump)
                        if nxt is not None:
                            for _ in nxt:
                                pass
                    sy.barrier()

            with ExitStack() as ph:
                Wo = T("Wo", [128, NCH, D], BF16, stack=ph)
                wos = [T("wos%d" % i, [128, D], stack=ph) for i in range(2)]
                g_bc = T("g_bc", [128, D], stack=ph)
                b_bc = T("b_bc", [128, D], stack=ph)
                ybuf = [T("ybuf%d" % i, [128, NCH, TT], BF16, stack=ph) for i in range(2)]
                xo = [T("xo%d" % i, [128, D], stack=ph) for i in range(2)]
                t1s = [T("t1_%d" % i, [128, D], stack=ph) for i in range(2)]
                ress = [T("res%d" % i, [128, D], stack=ph) for i in range(4)]
                xo2 = [T("xo2_%d" % i, [128, D], stack=ph) for i in range(2)]
                rr = T("rr", [128, 2, NB], stack=ph)
                st6 = T("st6o", [128, 4, 2, 6], stack=ph)
                mv = T("mvo", [128, 4, 2], stack=ph)
                rstd = T("rstdo", [128, 4], stack=ph)
                nmr = T("nmr", [128, 4], stack=ph)

                sy.dma("gb", lambda e: e.dma_start(out=g_bc[:], in_=ln_g[L, :, :].partition_broadcast(128)), writes=["g_bc"])
                sy.dma("gb2", lambda e: e.dma_start(out=b_bc[:], in_=ln_b[L, :, :].partition_broadcast(128)), writes=["b_bc"])
                for c in range(NCH):
                    buf = wos[c % 2]
                    key = "wos%d" % (c % 2)
                    sy.dma(key, lambda e: e.dma_start(out=buf[:], in_=w_out[L, c, :, :]), writes=[key])
                    op("dve" if c % 2 == 0 else "pool", lambda e: e.tensor_mul(out=Wo[:, c, :], in0=buf[:], in1=gate_bc[:]),
                       reads=[key, "gate_bc"], writes=[("Wo", c)])
                for br in range(2):
                    op("pool", lambda e: e.tensor_add(out=rr[:, br, :], in0=ss[:, br, :], in1=ss[:, 2 + br, :]),
                       reads=[("ss", 0, br), ("ss", 1, br)], writes=[("rr", br)])
                    op("act", lambda e: e.activation(out=rr[:, br, :], in_=rr[:, br, :], func=AF.Ln, scale=1.0 / 512.0, bias=EPS),
                       reads=[("rr", br)], writes=[("rr", br)])
                    op("act", lambda e: e.activation(out=rr[:, br, :], in_=rr[:, br, :], func=AF.Exp, scale=-0.5),
                       reads=[("rr", br)], writes=[("rr", br)])
                chunks = {0: (0, 1, 4, 5), 1: (2, 3, 6, 7)}
                def load_y(i):
                    yb = ybuf[i % 2]
                    ykey = "ybuf%d" % (i % 2)
                    sy.dma(ykey, lambda e: e.dma_start(out=yb[:], in_=yT[:, :, i * TT:(i + 1) * TT].rearrange("c p t -> p c t")),
                           reads=[("yTl", 0, i), ("yTl", 1, i), ("yTa", 0, i), ("yTa", 1, i)], writes=[ykey])

                def load_x(tbk):
                    xt = xo[tbk % 2]
                    xk = "xo%d" % (tbk % 2)
                    sy.dma(xk, lambda e: e.dma_start(out=xt[:], in_=x_src[tbk * 128:(tbk + 1) * 128, :]), reads=[(xs_key, tbk)], writes=[xk])

                def stA(tbk):
                    i, jb = tbk // 4, tbk % 4
                    yb = ybuf[i % 2]
                    ykey = "ybuf%d" % (i % 2)
                    if tbk == 0:
                        load_y(0)
                        load_x(0)
                    if jb == 0 and i + 1 < NT:
                        load_y(i + 1)
                    if tbk + 1 < NB:
                        load_x(tbk + 1)
                    xt = xo[tbk % 2]
                    xk = "xo%d" % (tbk % 2)
                    t1 = t1s[tbk % 2]
                    t1k = "t1_%d" % (tbk % 2)
                    res = ress[tbk % 4]
                    rk = "res%d" % (tbk % 4)
                    for nb in range(2):
                        for br in range(2):
                            bank = nb * 2 + br
                            for n_, c in enumerate(chunks[br]):
                                op("pe", lambda e: e.matmul(PSA[:, bank, :], lhsT=yb[:, c, jb * 128:(jb + 1) * 128], rhs=Wo[:, c, nb * 512:(nb + 1) * 512],
                                                            start=(n_ == 0), stop=(n_ == 3)),
                                   reads=[ykey, ("Wo", c)], writes=[bk(bank)])
                    for nb in range(2):
                        op("act", lambda e: e.activation(out=t1[:, nb * 512:(nb + 1) * 512], in_=PSA[:, nb * 2, :], func=AF.Copy, scale=rr[:, 0, tbk:tbk + 1]),
                           reads=[bk(nb * 2), ("rr", 0)], writes=[(t1k, nb)])
                        op("dve", lambda e: e.scalar_tensor_tensor(out=t1[:, nb * 512:(nb + 1) * 512], in0=PSA[:, nb * 2 + 1, :], scalar=rr[:, 1, tbk:tbk + 1],
                                                                   in1=t1[:, nb * 512:(nb + 1) * 512], op0=ALU.mult, op1=ALU.add),
                           reads=[bk(nb * 2 + 1), ("rr", 1), (t1k, nb)], writes=[(t1k, nb)])
                    op("dve", lambda e: e.scalar_tensor_tensor(out=res[:], in0=xt[:], scalar=alpha, in1=t1[:], op0=ALU.mult, op1=ALU.add),
                       reads=[xk, (t1k, 0), (t1k, 1)], writes=[rk])

                def stA2(tbk):
                    q3 = tbk % 4
                    res = ress[q3]
                    rk = "res%d" % q3
                    for hh in range(2):
                        op("dve", lambda e: e.bn_stats(out=st6[:, q3, hh, :], in_=res[:, hh * 512:(hh + 1) * 512]), reads=[rk], writes=[("st6o", q3, hh)])
                    op("dve", lambda e: e.bn_aggr(out=mv[:, q3, :], in_=st6[:, q3, :, :]), reads=[("st6o", q3, 0), ("st6o", q3, 1)], writes=[("mvo", q3)])

                def stB(tbk):
                    q3 = tbk % 4
                    res = ress[q3]
                    rk = "res%d" % q3
                    op("act", lambda e: e.activation(out=rstd[:, q3:q3 + 1], in_=mv[:, q3, 1:2], func=AF.Ln, bias=EPS), reads=[("mvo", q3)], writes=[("rstdo", q3)])
                    op("act", lambda e: e.activation(out=rstd[:, q3:q3 + 1], in_=rstd[:, q3:q3 + 1], func=AF.Exp, scale=-0.5), reads=[("rstdo", q3)], writes=[("rstdo", q3)])
                    op("dve", lambda e: e.scalar_tensor_tensor(out=nmr[:, q3:q3 + 1], in0=mv[:, q3, 0:1], scalar=-1.0, in1=rstd[:, q3:q3 + 1], op0=ALU.mult, op1=ALU.mult),
                       reads=[("mvo", q3), ("rstdo", q3)], writes=[("nmr", q3)])
                    op("act", lambda e: e.activation(out=res[:], in_=res[:], func=AF.Identity, scale=rstd[:, q3:q3 + 1], bias=nmr[:, q3:q3 + 1]),
                       reads=[rk, ("rstdo", q3), ("nmr", q3)], writes=[rk])

                def stC(tbk):
                    res = ress[tbk % 4]
                    rk = "res%d" % (tbk % 4)
                    xw = xo2[tbk % 2]
                    xwk = "xo2_%d" % (tbk % 2)
                    op("pool", lambda e: e.tensor_mul(out=res[:], in0=res[:], in1=g_bc[:]), reads=[rk, "g_bc"], writes=[rk])
                    op("pool", lambda e: e.tensor_add(out=xw[:], in0=res[:], in1=b_bc[:]), reads=[rk, "b_bc"], writes=[xwk])
                    sy.dma(xwk + "s", lambda e: e.dma_start(out=x_dst[tbk * 128:(tbk + 1) * 128, :], in_=xw[:]), reads=[xwk], writes=[(xd_key, tbk)])

                for n in range(NB + 3):
                    if n < NB:
                        stA(n)
                    if 0 <= n - 1 < NB:
                        stA2(n - 1)
                    if 0 <= n - 2 < NB:
                        stB(n - 2)
                    if 0 <= n - 3 < NB:
                        stC(n - 3)
                sy.barrier()
        sy.barrier()
    return nc


def make_inputs(inputs, S, NL):
    f = lambda a: np.ascontiguousarray(np.asarray(a, dtype=np.float32))
    w_in = f(inputs["w_in"])
    win_g = np.zeros((NL, 2, D, NCOL), np.float32)
    for g in range(2):
        sl = slice(g * 256, (g + 1) * 256)
        win_g[:, g, :, C_Q:C_Q + 256] = w_in[:NL, :, 1024:1536][:, :, sl]
        win_g[:, g, :, C_K:C_K + 256] = w_in[:NL, :, 1536:2048][:, :, sl]
        win_g[:, g, :, C_XR:C_XR + 256] = w_in[:NL, :, 0:512][:, :, sl]
        win_g[:, g, :, C_ZR:C_ZR + 256] = w_in[:NL, :, 512:1024][:, :, sl]
        win_g[:, g, :, C_ZA:C_ZA + 256] = w_in[:NL, :, 2560:3072][:, :, sl]
        win_g[:, g, :, C_V:C_V + 256] = w_in[:NL, :, 2048:2560][:, :, sl]
        win_g[:, g, :, C_FL:C_FL + 4] = w_in[:NL, :, 3072 + g * 4:3072 + (g + 1) * 4]
    conv_w, conv_b = f(inputs["conv_w"]), f(inputs["conv_b"])
    b_a, b_x, lam = f(inputs["b_gate_a"]), f(inputs["b_gate_x"]), f(inputs["lru_lambda"])
    n_l, n_a = f(inputs["norm_lru"]), f(inputs["norm_attn"])
    chanvec = np.zeros((NL, 2, 128, 20), np.float32)
    wga, wgx = f(inputs["w_gate_a"]), f(inputs["w_gate_x"])
    w_gate = np.zeros((NL, 2, 4, 128, 128), np.float32)
    for g in range(2):
        for cg in range(2):
            ch = slice(g * 256 + cg * 128, g * 256 + (cg + 1) * 128)
            o = cg * 9
            for k in range(4):
                chanvec[:, g, :, o + k] = conv_w[:NL, k, ch]
            chanvec[:, g, :, o + 4] = conv_b[:NL, ch]
            chanvec[:, g, :, o + 5] = b_a[:NL, ch]
            chanvec[:, g, :, o + 6] = b_x[:NL, ch]
            chanvec[:, g, :, o + 7] = lam[:NL, ch]
            chanvec[:, g, :, o + 8] = n_l[:NL, ch]
            chanvec[:, g, :, 18 + cg] = n_a[:NL, ch]
            for bl in range(2):
                blk = (g * 256 + cg * 128) // 64 + bl
                w_gate[:, g, cg * 2 + 0, bl * 64:(bl + 1) * 64, bl * 64:(bl + 1) * 64] = wga[:NL, blk]
                w_gate[:, g, cg * 2 + 1, bl * 64:(bl + 1) * 64, bl * 64:(bl + 1) * 64] = wgx[:NL, blk]
    w_out = f(inputs["w_out"])
    wout_r = np.zeros((NL, NCH, 128, D), np.float32)
    for g in range(2):
        for j in range(4):
            if j < 2:
                r0 = g * 256 + j * 128
            else:
                r0 = 512 + g * 256 + (j - 2) * 128
            wout_r[:, g * 4 + j] = w_out[:NL, r0:r0 + 128, :]
    b_fg = f(inputs["b_fgate"])[:NL].reshape(NL, 2, NH, 1)
    common = {
        "w_ada": f(inputs["w_ada"])[:NL],
        "b_ada": f(inputs["b_ada"])[:NL].reshape(NL, 1, 3 * D),
        "w_in": win_g, "b_fg": np.ascontiguousarray(b_fg), "chanvec": chanvec, "w_gate": w_gate, "w_out": wout_r,
        "ln_g": f(inputs["ln_gain"])[:NL].reshape(NL, 1, D), "ln_b": f(inputs["ln_bias"])[:NL].reshape(NL, 1, D),
        "ident": np.eye(128, dtype=np.float32),
        "tri": np.where(np.arange(128)[:, None] <= np.arange(128)[None, :], 0.0, NEG).astype(np.float32),
    }
    x = f(inputs["x"])
    c = f(inputs["c"])
    maps = []
    for b in range(x.shape[0]):
        m = dict(common)
        m["x"] = np.ascontiguousarray(x[b, :S])
        m["c_col"] = np.ascontiguousarray(c[b].reshape(NCH, 128).T)
        maps.append(m)
    return maps


_CACHE = {}


def run(inputs, S=8192, NL=2):
    key = (S, NL)
    if key not in _CACHE:
        _CACHE[key] = build_program(S, NL)
    nc = _CACHE[key]
    maps = make_inputs(inputs, S, NL)
    res = run_bass_kernel_spmd(nc, maps, core_ids=list(range(len(maps))))
    return np.stack([r["y"] for r in res.results], axis=0)


def kernel(**inputs):
    return run(inputs, S=8192, NL=2).astype(np.float32)
```

```python
import numpy as np
from contextlib import ExitStack

import concourse.bass as bass
import concourse.mybir as mybir
from concourse.bass_utils import run_bass_kernel_spmd

F32 = mybir.dt.float32
BF16 = mybir.dt.bfloat16
AF = mybir.ActivationFunctionType
ALU = mybir.AluOpType
AX = mybir.AxisListType

D = 1024
NCH = 8
TT = 512
NH = 4
DH = 64
NCOL = 1540
C_Q, C_K, C_XR, C_ZR, C_ZA, C_V, C_FL = 0, 256, 512, 768, 1024, 1280, 1536
EPS = 1e-5
NEG = -30000.0
N_CORES = 4


class Sync:
    def __init__(self, nc, es):
        self.nc = nc
        self.es = es
        self.engs = {"pe": nc.tensor, "act": nc.scalar, "dve": nc.vector, "pool": nc.gpsimd, "sp": nc.sync}
        self.sem = {}
        self.cnt = {}
        for k in ("pe", "act", "dve", "pool"):
            self.sem[k] = es.enter_context(nc.semaphore("s_" + k))
            self.cnt[k] = 0
        self.seen = {k: {} for k in self.engs}
        self.lw = {}
        self.rd = {}
        self.n_wait = 0

    def dma_slot(self, name):
        src = "dma:" + name
        if src not in self.sem:
            self.sem[src] = self.es.enter_context(self.nc.semaphore("d_" + name))
            self.cnt[src] = 0
        return src

    def _wait(self, E, src, val):
        if val <= 0 or self.seen[E].get(src, 0) >= val:
            return
        self.engs[E].wait_ge(self.sem[src], val)
        self.seen[E][src] = val
        self.n_wait += 1

    def _deps(self, E, reads, writes):
        deps = {}

        def add(src, val):
            if deps.get(src, 0) < val:
                deps[src] = val

        for k in reads:
            w = self.lw.get(k)
            if w is not None:
                add(*w)
        for k in writes:
            w = self.lw.get(k)
            if w is not None:
                add(*w)
            for src, val in self.rd.get(k, {}).items():
                add(src, val)
        if E == "pe":
            deps.pop("pe", None)
        for src, val in deps.items():
            self._wait(E, src, val)

    @staticmethod
    def _norm(keys):
        out = []
        for k in keys:
            if k == "misc":
                k = ("ps", 6)
            elif k in ("pst", "pss"):
                k = ("ps", 7)
            out.append(k)
        return out

    def op(self, E, fn, reads=(), writes=()):
        reads, writes = self._norm(reads), self._norm(writes)
        writes = list(writes) + [k for k in reads if isinstance(k, tuple) and k[0] == "ps" and k not in writes]
        self._deps(E, reads, writes)
        ins = fn(self.engs[E])
        self.cnt[E] += 1
        ins.then_inc(self.sem[E], 1)
        val = self.cnt[E]
        for k in writes:
            self.lw[k] = (E, val)
            self.rd[k] = {}
        for k in reads:
            self.rd.setdefault(k, {})[E] = val
        return ins

    def dma(self, slot, fn, reads=(), writes=(), q="sp", first=True):
        src = self.dma_slot(slot)
        if first:
            self._wait(q, src, 16 * self.cnt[src])
        self._deps(q, reads, writes)
        ins = fn(self.engs[q])
        self.cnt[src] += 1
        ins.then_inc(self.sem[src], 16)
        val = 16 * self.cnt[src]
        for k in writes:
            self.lw[k] = (src, val)
            self.rd[k] = {}
        for k in reads:
            self.rd.setdefault(k, {})[src] = val

    def barrier(self):
        for E in ("pe", "act", "dve", "pool", "sp"):
            for src in list(self.sem.keys()):
                if src == E:
                    continue
                v = self.cnt[src] * (16 if src.startswith("dma:") else 1)
                self._wait(E, src, v)


def build_program(S, NL, debug=False):
    NT = S // TT
    NB = S // 128
    nc = bass.Bass("TRN2", target_bir_lowering=False)

    def din(name, shape, dt=F32):
        return nc.dram_tensor(name, list(shape), dt, kind="ExternalInput").ap()

    x_in = din("x", [S, D])
    c_col = din("c_col", [128, NCH])
    w_ada = din("w_ada", [NL, D, 3 * D])
    b_ada = din("b_ada", [NL, 1, 3 * D])
    w_in = din("w_in", [NL, 2, D, NCOL])
    b_fg = din("b_fg", [NL, 2, NH, 1])
    chanvec = din("chanvec", [NL, 2, 128, 20])
    w_gate = din("w_gate", [NL, 2, 4, 128, 128])
    w_out = din("w_out", [NL, NCH, 128, D])
    ln_g = din("ln_g", [NL, 1, D])
    ln_b = din("ln_b", [NL, 1, D])
    ident_in = din("ident", [128, 128])
    tri_in = din("tri", [128, 128])
    y_out = nc.dram_tensor("y", [S, D], F32, kind="ExternalOutput").ap()
    x_mid = nc.dram_tensor("x_mid", [S, D], F32).ap()
    yT = nc.dram_tensor("yT", [NCH, 128, S], BF16).ap()
    dbg = {}

    alpha = float((2.0 * 2) ** 0.25)

    with ExitStack() as es:
        sy = Sync(nc, es)
        op = sy.op

        uniq = [0]

        def T(name, shape, dt=F32, stack=es):
            uniq[0] += 1
            return stack.enter_context(nc.sbuf_tensor("%s_%d" % (name, uniq[0]), list(shape), dt))

        PSA = es.enter_context(nc.psum_tensor("psa", [128, 6, 512], F32))
        MISC = es.enter_context(nc.psum_tensor("misc", [128, 512], F32))
        B7 = es.enter_context(nc.psum_tensor("b7", [128, 512], F32))
        PST = B7[:, 0:256].bitcast(BF16).rearrange("p (c t) -> p c t", c=4)
        PSS = B7[:, 256:288]

        def bk(n):
            return ("ps", n)

        ident_f = T("ident_f", [128, 128])
        tri_f = T("tri_f", [128, 128])
        ident = T("ident_b", [128, 128], BF16)
        tri = T("tri_b", [128, 128], BF16)
        ones_f = T("ones_f", [128, 128])
        ones_b = T("ones_b", [128, 1], BF16)
        cact = T("cact", [128, NCH])
        s1_col = T("s1_col", [128, NCH])
        sh_col = T("sh_col", [128, NCH])
        gate_bc = T("gate_bc", [128, D])
        ss = T("ss", [128, 4, NB])

        sy.dma("c0", lambda e: e.dma_start(out=ident_f[:], in_=ident_in[:, :]), writes=["ident_f"])
        sy.dma("c1", lambda e: e.dma_start(out=tri_f[:], in_=tri_in[:, :]), writes=["tri_f"])
        sy.dma("c2", lambda e: e.dma_start(out=cact[:], in_=c_col[:, :]), writes=["cact"])
        op("dve", lambda e: e.tensor_copy(out=ident[:], in_=ident_f[:]), reads=["ident_f"], writes=["ident"])
        op("dve", lambda e: e.tensor_copy(out=tri[:], in_=tri_f[:]), reads=["tri_f"], writes=["tri"])
        op("dve", lambda e: e.memset(ones_f[:], 1.0), writes=["ones_f"])
        op("dve", lambda e: e.memset(ones_b[:], 1.0), writes=["ones_b"])
        with ExitStack() as ph:
            ce = T("ce", [128, NCH], stack=ph)
            op("act", lambda e: e.activation(out=ce[:], in_=cact[:], func=AF.Exp, scale=-1.0), reads=["cact"], writes=["ce"])
            op("dve", lambda e: e.tensor_scalar_add(out=ce[:], in0=ce[:], scalar1=1.0), reads=["ce"], writes=["ce"])
            op("dve", lambda e: e.reciprocal(out=ce[:], in_=ce[:]), reads=["ce"], writes=["ce"])
            op("dve", lambda e: e.tensor_mul(out=cact[:], in0=cact[:], in1=ce[:]), reads=["ce", "cact"], writes=["cact"])
            sy.barrier()

        for L in range(NL):
            x_src = x_in if L == 0 else x_mid
            x_dst = y_out if L == NL - 1 else x_mid
            xs_key = "x_in" if L == 0 else "x_mid"
            xd_key = "y_out" if L == NL - 1 else "x_mid"

            with ExitStack() as ph:
                wa = [T("wa%d" % i, [128, 3 * D], stack=ph) for i in range(2)]
                bst = [T("bst%d" % i, [128, 512], stack=ph) for i in range(2)]
                s1_bc = T("s1_bc", [128, D], stack=ph)
                sh_bc = T("sh_bc", [128, D], stack=ph)
                dtmp = T("dtmp", [128, 128], stack=ph)
                for c in range(NCH):
                    buf = wa[c % 2]
                    key = "wa%d" % (c % 2)
                    sy.dma(key, lambda e: e.dma_start(out=buf[:], in_=w_ada[L, c * 128:(c + 1) * 128, :]), writes=[key])
                    for nb in range(6):
                        op("pe", lambda e: e.matmul(PSA[:, nb, :], lhsT=cact[:, c:c + 1].to_broadcast([128, 128]), rhs=buf[:, nb * 512:(nb + 1) * 512],
                                                    start=(c == 0), stop=(c == NCH - 1)),
                           reads=[key, "cact"], writes=[bk(nb)])
                for nb in range(6):
                    vi, hb = nb // 2, nb % 2
                    dst, dkey, addc = ((sh_bc, "sh_bc", 0.0), (s1_bc, "s1_bc", 1.0), (gate_bc, "gate_bc", 0.0))[vi]
                    bb = bst[nb % 2]
                    bkey = "bst%d" % (nb % 2)
                    sy.dma(bkey, lambda e: e.dma_start(out=bb[:], in_=b_ada[L, :, nb * 512:(nb + 1) * 512].partition_broadcast(128)), writes=[bkey])
                    op("dve", lambda e: e.scalar_tensor_tensor(out=dst[:, hb * 512:(hb + 1) * 512], in0=PSA[:, nb, :], scalar=addc, in1=bb[:],
                                                               op0=ALU.add, op1=ALU.add),
                       reads=[bk(nb), bkey], writes=[dkey])
                for src, skey, dstc, dck in ((s1_bc, "s1_bc", s1_col, "s1_col"), (sh_bc, "sh_bc", sh_col, "sh_col")):
                    for c in range(NCH):
                        op("dve", lambda e: e.tensor_mul(out=dtmp[:], in0=src[:, c * 128:(c + 1) * 128], in1=ident_f[:]),
                           reads=[skey, "ident_f"], writes=["dtmp"])
                        op("dve", lambda e: e.reduce_sum(out=dstc[:, c:c + 1], in_=dtmp[:], axis=AX.X), reads=["dtmp"], writes=[dck])
                sy.barrier()

            for g in range(2):
                with ExitStack() as ph:
                    Wp = T("Wp", [128, NCH, NCOL], BF16, stack=ph)
                    Kc = [T("Kc%d" % h, [128, S], BF16, stack=ph) for h in range(NH)]
                    Vc = T("Vc", [128, NB, NH, DH + 1], BF16, stack=ph)
                    Qa = [[T("Qa%d_%d" % (p_, h), [128, TT], BF16, stack=ph) for h in range(NH)] for p_ in range(2)]
                    Wg = T("Wg", [128, 4, 128], BF16, stack=ph)
                    cv = T("cv", [128, 20], stack=ph)
                    bcol = T("bcol", [128, 13], stack=ph)
                    nbcol = T("nbcol", [128, 12], stack=ph)
                    bv_bc = T("bv_bc", [128, 256], stack=ph)
                    cvd = T("cvd", [128, 8], stack=ph)
                    nbf = T("nbf", [NH, 1], stack=ph)

                    with ExitStack() as ph2:
                        wst = [T("wst%d" % i, [128, NCOL], stack=ph2) for i in range(2)]
                        wgs = T("wgs", [128, 4, 128], stack=ph2)
                        brow_s = T("brow_s", [1, NCOL], stack=ph2)
                        for c in range(NCH):
                            buf = wst[c % 2]
                            key = "wst%d" % (c % 2)
                            sy.dma(key, lambda e: e.dma_start(out=buf[:], in_=w_in[L, g, c * 128:(c + 1) * 128, :]), writes=[key])
                            for nb4 in range(4):
                                c0, c1 = nb4 * 512, min(NCOL, (nb4 + 1) * 512)
                                op("pe", lambda e: e.matmul(PSA[0:1, nb4, 0:c1 - c0], lhsT=sh_col[:, c:c + 1], rhs=buf[:, c0:c1],
                                                            start=(c == 0), stop=(c == NCH - 1)),
                                   reads=[key, "sh_col"], writes=[bk(nb4)])
                            if c % 2 == 0:
                                op("dve", lambda e: e.tensor_scalar(out=Wp[:, c, :], in0=buf[:], scalar1=s1_col[:, c:c + 1], scalar2=None, op0=ALU.mult),
                                   reads=[key, "s1_col"], writes=[("Wp", c)])
                            else:
                                op("pool", lambda e: e.tensor_scalar(out=Wp[:, c, :], in0=buf[:], scalar1=s1_col[:, c:c + 1], scalar2=0.0,
                                                                     op0=ALU.mult, op1=ALU.add),
                                   reads=[key, "s1_col"], writes=[("Wp", c)])
                        for nb4 in range(4):
                            c0, c1 = nb4 * 512, min(NCOL, (nb4 + 1) * 512)
                            op("dve", lambda e: e.tensor_copy(out=brow_s[0:1, c0:c1], in_=PSA[0:1, nb4, 0:c1 - c0]), reads=[bk(nb4)], writes=["brow_s"])
                        for m in range(13):
                            mm = 128 if m < 12 else NH
                            op("pe", lambda e: e.matmul(MISC[0:mm, m:m + 1], lhsT=brow_s[0:1, m * 128:m * 128 + mm], rhs=ones_f[0:1, 0:1],
                                                        start=True, stop=True), reads=["brow_s", "ones_f"], writes=["misc"])
                        op("dve", lambda e: e.tensor_copy(out=bcol[:, 0:12], in_=MISC[:, 0:12]), reads=["misc"], writes=["bcol"])
                        op("dve", lambda e: e.tensor_copy(out=bcol[0:NH, 12:13], in_=MISC[0:NH, 12:13]), reads=["misc"], writes=["bcol"])
                        op("dve", lambda e: e.tensor_scalar(out=nbcol[:, 0:12], in0=MISC[:, 0:12], scalar1=-1.0, scalar2=None, op0=ALU.mult),
                           reads=["misc"], writes=["nbcol"])
                        op("pe", lambda e: e.matmul(PSA[:, 4, 0:256], lhsT=ones_f[0:1, :], rhs=brow_s[0:1, C_V:C_V + 256], start=True, stop=True),
                           reads=["brow_s", "ones_f"], writes=[bk(4)])
                        op("dve", lambda e: e.tensor_copy(out=bv_bc[:], in_=PSA[:, 4, 0:256]), reads=[bk(4)], writes=["bv_bc"])
                        sy.dma("wgs", lambda e: e.dma_start(out=wgs[:], in_=w_gate[L, g].rearrange("k p m -> p k m")), writes=["wgs"])
                        op("pool", lambda e: e.tensor_copy(out=Wg[:], in_=wgs[:]), reads=["wgs"], writes=["Wg"])
                        sy.barrier()
                    xb = [T("xb%d" % i, [128, D], stack=ph) for i in range(2)]
                    hb16 = T("hb16", [128, D], BF16, stack=ph)
                    hT = T("hT", [128, NCH, TT], BF16, stack=ph)
                    st6 = T("st6", [128, 4, 2, 6], stack=ph)
                    mv = T("mv", [128, 4, 2], stack=ph)
                    rstd = T("rstd", [128, 4], stack=ph)
                    xr = [T("xr%d" % i, [128, TT + 3], stack=ph) for i in range(2)]
                    hcar = T("hcar", [128, 2], stack=ph)
                    Gr = T("Gr", [128, 2, TT], stack=ph)
                    Ga = [T("Ga%d" % p_, [128, 2, TT], stack=ph) for p_ in range(2)]
                    ta = T("ta", [128, TT], stack=ph)
                    tb_ = T("tb", [128, TT], stack=ph)
                    tcc = T("tc", [128, TT], stack=ph)
                    td = T("td", [128, TT], stack=ph)
                    te = T("te", [128, TT], stack=ph)
                    tf = T("tf", [128, TT], stack=ph)
                    xcb = T("xcb", [128, TT], BF16, stack=ph)
                    sqb = T("sqb", [128, TT], BF16, stack=ph)
                    ysl = [T("ysl%d" % p_, [128, 2, TT], BF16, stack=ph) for p_ in range(2)]
                    ysa = T("ysa", [128, 2, TT], BF16, stack=ph)
                    fin1 = T("fin1", [128, TT], stack=ph)
                    fin2 = T("fin2", [128, TT], stack=ph)
                    PT = [T("PT%d" % i, [128, TT], BF16, stack=ph) for i in range(4)]
                    sz = T("sz", [128, TT], stack=ph)
                    se = T("se", [128, TT], stack=ph)
                    flv = Kc[0][96:96 + NH, :].bitcast(F32)
                    fle, nd, r1, onesr = flv[:, 0:TT], flv[:, TT:2 * TT], flv[:, 2 * TT:3 * TT], flv[:, 3 * TT:4 * TT]
                    ndc = Kc[1][96:96 + NH, 3 * TT:3 * TT + 2].bitcast(F32)
                    spl = Kc[1][96:96 + NH, 0:3 * TT].rearrange("p (a t) -> p a t", a=3)
                    sqa = T("sqa", [128, TT], BF16, stack=ph)
                    pss_sb = T("pss_sb", [128, 24], stack=ph)

                    sy.dma("cv", lambda e: e.dma_start(out=cv[:], in_=chanvec[L, g, :, :]), writes=["cv"])
                    sy.dma("nbf", lambda e: e.dma_start(out=nbf[:], in_=b_fg[L, g, :, :]), writes=["nbf"])
                    op("dve", lambda e: e.tensor_add(out=nbf[:], in0=nbf[:], in1=bcol[0:NH, 12:13]), reads=["nbf", "bcol"], writes=["nbf"])
                    op("dve", lambda e: e.tensor_scalar(out=nbf[:], in0=nbf[:], scalar1=-1.0, scalar2=None, op0=ALU.mult), reads=["nbf"], writes=["nbf"])
                    for cg in range(2):
                        o = cg * 9
                        op("dve", lambda e: e.tensor_scalar_mul(out=cvd[:, cg * 4:cg * 4 + 2], in0=cv[:, o + 5:o + 7], scalar1=-1.0),
                           reads=["cv"], writes=["cvd"])
                        op("act", lambda e: e.activation(out=cvd[:, cg * 4 + 2:cg * 4 + 3], in_=cv[:, o + 7:o + 8], func=AF.Exp, scale=-1.0),
                           reads=["cv", "cvd"], writes=["cvd"])
                        op("act", lambda e: e.activation(out=cvd[:, cg * 4 + 2:cg * 4 + 3], in_=cvd[:, cg * 4 + 2:cg * 4 + 3], func=AF.Ln, bias=1.0),
                           reads=["cvd"], writes=["cvd"])
                        op("dve", lambda e: e.tensor_scalar_mul(out=cvd[:, cg * 4 + 3:cg * 4 + 4], in0=cvd[:, cg * 4 + 2:cg * 4 + 3], scalar1=-16.0),
                           reads=["cvd"], writes=["cvd"])
                        op("dve", lambda e: e.tensor_scalar_mul(out=cvd[:, cg * 4 + 2:cg * 4 + 3], in0=cvd[:, cg * 4 + 2:cg * 4 + 3], scalar1=-8.0),
                           reads=["cvd"], writes=["cvd"])
                        op("pool", lambda e: e.memset(xr[cg][:, 0:3], 0.0), writes=[("xr", cg)])
                    op("pool", lambda e: e.memset(hcar[:], 0.0), writes=["hcar"])
                    op("pool", lambda e: e.memset(ndc, 0.0), writes=["ndc"])
                    op("pool", lambda e: e.memset(onesr, 1.0), writes=["onesr"])
                    op("pool", lambda e: e.memset(Vc[:, :, :, DH:DH + 1], 1.0), writes=["Vones"])
                    for h in range(NH):
                        op("pool", lambda e: e.memset(Kc[h][64:70, :], -1.0), writes=[("Kones", h)])
                        for p_ in range(2):
                            op("pool", lambda e: e.memset(Qa[p_][h][64:70, :], 1.0), writes=[("Qa", p_, h)])

                    prot = [0]
                    pcur = [None, None]

                    def pbank(i):
                        if i <= 8:
                            prot[0] += 1
                            n = (6, 1, 3)[prot[0] % 3]
                        else:
                            n = 6
                        ap = MISC if n == 6 else PSA[:, n, :]
                        pcur[0], pcur[1] = ap, bk(n)
                        return ap, bk(n)

                    def proj(i, col0, ncols, M=128):
                        ap, key = pbank(i)
                        for c in range(NCH):
                            op("pe", lambda e: e.matmul(ap[0:M, :], lhsT=Wp[:, c, col0:col0 + ncols], rhs=hT[:, c, :],
                                                        start=(c == 0), stop=(c == NCH - 1)),
                               reads=[("Wp", c), "hT"], writes=[key])
                        return ap, key

                    def pre(i):
                        par = i % 2
                        def ln_a(j):
                            blk = i * 4 + j
                            xt = xb[blk % 2]
                            xk = "xb%d" % (blk % 2)
                            sy.dma(xk, lambda e: e.dma_start(out=xt[:], in_=x_src[blk * 128:(blk + 1) * 128, :]),
                                   reads=[(xs_key, blk)], writes=[xk])
                            for hh in range(2):
                                op("dve", lambda e: e.bn_stats(out=st6[:, j, hh, :], in_=xt[:, hh * 512:(hh + 1) * 512]),
                                   reads=[xk], writes=[("st6", j, hh)])
                            op("dve", lambda e: e.bn_aggr(out=mv[:, j, :], in_=st6[:, j, :, :]), reads=[("st6", j, 0), ("st6", j, 1)], writes=[("mv", j)])
                            op("act", lambda e: e.activation(out=rstd[:, j:j + 1], in_=mv[:, j, 1:2], func=AF.Ln, bias=EPS), reads=[("mv", j)], writes=[("rstd", j)])
                            op("act", lambda e: e.activation(out=rstd[:, j:j + 1], in_=rstd[:, j:j + 1], func=AF.Exp, scale=-0.5), reads=[("rstd", j)], writes=[("rstd", j)])

                        def ln_b(j):
                            blk = i * 4 + j
                            xt = xb[blk % 2]
                            xk = "xb%d" % (blk % 2)
                            op("dve", lambda e: e.tensor_scalar(out=hb16[:], in0=xt[:], scalar1=mv[:, j, 0:1], scalar2=rstd[:, j:j + 1],
                                                                op0=ALU.subtract, op1=ALU.mult),
                               reads=[xk, ("mv", j), ("rstd", j)], writes=["hb16"])
                            if i <= 8:
                                n = 1 if j % 2 == 0 else 3
                                tp8 = PSA[:, n, :].bitcast(BF16).rearrange("p (c t) -> p c t", c=8)
                                for c in range(NCH):
                                    op("pe", lambda e: e.transpose(out=tp8[:, c, :], in_=hb16[:, c * 128:(c + 1) * 128], identity=ident[:]),
                                       reads=["hb16", "ident"], writes=[bk(n)])
                                op("dve", lambda e: e.tensor_copy(out=hT[:, :, j * 128:(j + 1) * 128], in_=tp8), reads=[bk(n)], writes=["hT"])
                            else:
                                for half in range(2):
                                    for cc in range(4):
                                        c = half * 4 + cc
                                        op("pe", lambda e: e.transpose(out=PST[:, cc, :], in_=hb16[:, c * 128:(c + 1) * 128], identity=ident[:]),
                                           reads=["hb16", "ident"], writes=["pst"])
                                    op("dve", lambda e: e.tensor_copy(out=hT[:, half * 4:half * 4 + 4, j * 128:(j + 1) * 128], in_=PST),
                                       reads=["pst"], writes=["hT"])
                        ln_a(0)
                        yield
                        ln_a(1)
                        yield
                        for j in range(4):
                            ln_b(j)
                            yield
                            if j + 2 < 4:
                                ln_a(j + 2)
                                yield
                        for cg in range(2):
                            pa, pk = proj(i, C_XR + cg * 128, 128)
                            op("dve", lambda e: e.tensor_scalar(out=xr[cg][:, 3:TT + 3], in0=pa[:, :], scalar1=bcol[:, 4 + cg:5 + cg], scalar2=None, op0=ALU.add),
                               reads=[pk, "bcol"], writes=[("xr", cg)])
                            yield
                        gr_ready = [False, False]

                        def chain_b():
                            def silu(kind, cg):
                                pa, pk = proj(i, (C_ZR if kind == 0 else C_ZA) + cg * 128, 128)
                                Gt = Gr if kind == 0 else Ga[par]
                                gk = ("Gr", cg) if kind == 0 else ("Ga", par, cg)
                                gcol = (cg * 9 + 8) if kind == 0 else (18 + cg)
                                mcol = (6 if kind == 0 else 8) + cg
                                op("dve", lambda e: e.tensor_scalar(out=sz[:], in0=pa[:, :], scalar1=bcol[:, mcol:mcol + 1], scalar2=None, op0=ALU.add),
                                   reads=[pk, "bcol"], writes=["sz"])
                                yield
                                op("act", lambda e: e.activation(out=se[:], in_=sz[:], func=AF.Exp, scale=-1.0), reads=["sz"], writes=["se"])
                                op("act", lambda e: e.activation(out=se[:], in_=se[:], func=AF.Ln, bias=1.0), reads=["se"], writes=["se"])
                                op("act", lambda e: e.activation(out=se[:], in_=se[:], func=AF.Exp, scale=-1.0), reads=["se"], writes=["se"])
                                op("pool", lambda e: e.tensor_scalar(out=sz[:], in0=sz[:], scalar1=cv[:, gcol:gcol + 1], scalar2=0.0, op0=ALU.mult, op1=ALU.add),
                                   reads=["sz", "cv"], writes=["sz"])
                                yield
                                op("pool", lambda e: e.tensor_mul(out=Gt[:, cg, :], in0=sz[:], in1=se[:]), reads=["se", "sz"], writes=[gk])
                                if kind == 0:
                                    gr_ready[cg] = True
                                yield
                            for cg in range(2):
                                yield from silu(0, cg)
                            for hp in range(2):
                                pa, pk = proj(i, C_Q + hp * 128, 128)
                                for s_ in range(2):
                                    h = hp * 2 + s_
                                    op("dve", lambda e: e.tensor_scalar(out=Qa[par][h][0:64, :], in0=pa[s_ * 64:(s_ + 1) * 64, :],
                                                                        scalar1=bcol[s_ * 64:(s_ + 1) * 64, hp:hp + 1], scalar2=0.125, op0=ALU.add, op1=ALU.mult),
                                       reads=[pk, "bcol"], writes=[("Qa", par, h)])
                                yield
                            for hp in range(2):
                                pa, pk = proj(i, C_K + hp * 128, 128)
                                for s_ in range(2):
                                    h = hp * 2 + s_
                                    op("dve", lambda e: e.tensor_scalar(out=Kc[h][0:64, i * TT:(i + 1) * TT], in0=pa[s_ * 64:(s_ + 1) * 64, :],
                                                                        scalar1=bcol[s_ * 64:(s_ + 1) * 64, 2 + hp:3 + hp], scalar2=None, op0=ALU.add),
                                       reads=[pk, "bcol"], writes=[("K", h, i)])
                                yield
                            pa, pk = pbank(i)
                            for c in range(NCH):
                                op("pe", lambda e: e.matmul(pa[0:NH, :], lhsT=Wp[:, c, C_FL:C_FL + NH], rhs=hT[:, c, :],
                                                            start=(c == 0), stop=(c == NCH - 1)),
                                   reads=[("Wp", c), "hT"], writes=[pk])
                            op("act", lambda e: e.activation(out=fle, in_=pa[0:NH, :], func=AF.Exp, scale=-1.0, bias=nbf[:, 0:1]),
                               reads=[pk, "nbf"], writes=["fle"])
                            yield
                            op("act", lambda e: e.activation(out=fle, in_=fle, func=AF.Ln, bias=1.0), reads=["fle"], writes=["fle"])
                            yield
                            op("dve", lambda e: e.tensor_tensor_scan(out=nd, data0=onesr, data1=fle, initial=ndc,
                                                                     op0=ALU.mult, op1=ALU.add),
                               reads=["fle", "ndc", "onesr"], writes=["nd"])
                            op("dve", lambda e: e.tensor_copy(out=ndc, in_=nd[:, TT - 1:TT]), reads=["nd"], writes=["ndc"])
                            yield
                            op("dve", lambda e: e.tensor_copy(out=spl[:, 0, :], in_=nd), reads=["nd"], writes=["spl"])
                            op("dve", lambda e: e.tensor_sub(out=r1, in0=nd, in1=spl[:, 0, :]), reads=["nd", "spl"], writes=["r1"])
                            yield
                            op("dve", lambda e: e.tensor_copy(out=spl[:, 1, :], in_=r1), reads=["r1"], writes=["spl"])
                            op("dve", lambda e: e.tensor_sub(out=r1, in0=r1, in1=spl[:, 1, :]), reads=["r1", "spl"], writes=["r1"])
                            yield
                            op("dve", lambda e: e.tensor_copy(out=spl[:, 2, :], in_=r1), reads=["r1"], writes=["spl"])
                            yield
                            for h in range(NH):
                                sy.dma("augK%d" % h, lambda e: e.dma_start(out=Kc[h][64:67, i * TT:(i + 1) * TT], in_=spl[h:h + 1, :, :]),
                                       reads=["spl", ("Kones", h)], writes=[("Kaug", h, i)])
                                sy.dma("augQ%d" % h, lambda e: e.dma_start(out=Qa[par][h][67:70, :], in_=spl[h:h + 1, :, :]),
                                       reads=["spl"], writes=[("Qa", par, h)])
                            yield
                            for cg in range(2):
                                yield from silu(1, cg)
                            for jb in range(4):
                                pa, pk = pbank(i)
                                for c in range(NCH):
                                    op("pe", lambda e: e.matmul(pa[:, 0:256], lhsT=hT[:, c, jb * 128:(jb + 1) * 128], rhs=Wp[:, c, C_V:C_V + 256],
                                                                start=(c == 0), stop=(c == NCH - 1)),
                                       reads=[("Wp", c), "hT"], writes=[pk])
                                op("dve", lambda e: e.tensor_tensor(out=Vc[:, i * 4 + jb, :, 0:DH],
                                                                    in0=pa[:, 0:256].rearrange("p (h d) -> p h d", h=NH),
                                                                    in1=bv_bc[:].rearrange("p (h d) -> p h d", h=NH), op=ALU.add),
                                   reads=[pk, "bv_bc"], writes=[("V", i)])
                                yield

                        def chain_a():
                            for cg in range(2):
                                o = cg * 9
                                xk = ("xr", cg)
                                X = xr[cg]
                                op("dve", lambda e: e.tensor_scalar(out=ta[:], in0=X[:, 0:TT], scalar1=cv[:, o:o + 1], scalar2=cv[:, o + 4:o + 5],
                                                                    op0=ALU.mult, op1=ALU.add), reads=[xk, "cv"], writes=["ta"])
                                for k in range(1, 4):
                                    op("dve", lambda e: e.scalar_tensor_tensor(out=ta[:], in0=X[:, k:k + TT], scalar=cv[:, o + k:o + k + 1], in1=ta[:],
                                                                               op0=ALU.mult, op1=ALU.add), reads=[xk, "cv", "ta"], writes=["ta"])
                                yield
                                op("pool", lambda e: e.tensor_copy(out=X[:, 0:3], in_=X[:, TT:TT + 3]), reads=[xk], writes=[xk])
                                op("dve", lambda e: e.tensor_copy(out=xcb[:], in_=ta[:]), reads=["ta"], writes=["xcb"])
                                yield
                                op("pe", lambda e: e.matmul(MISC[:, :], lhsT=Wg[:, cg * 2, :], rhs=xcb[:], start=True, stop=True),
                                   reads=["Wg", "xcb"], writes=["misc"])
                                op("act", lambda e: e.activation(out=tb_[:], in_=MISC[:, :], func=AF.Exp, scale=-1.0, bias=cvd[:, cg * 4:cg * 4 + 1]),
                                   reads=["misc", "cvd"], writes=["tb"])
                                yield
                                op("pe", lambda e: e.matmul(MISC[:, :], lhsT=Wg[:, cg * 2 + 1, :], rhs=xcb[:], start=True, stop=True),
                                   reads=["Wg", "xcb"], writes=["misc"])
                                op("act", lambda e: e.activation(out=tcc[:], in_=MISC[:, :], func=AF.Exp, scale=-1.0, bias=cvd[:, cg * 4 + 1:cg * 4 + 2]),
                                   reads=["misc", "cvd"], writes=["tc"])
                                op("act", lambda e: e.activation(out=tb_[:], in_=tb_[:], func=AF.Ln, bias=1.0), reads=["tb"], writes=["tb"])
                                yield
                                op("act", lambda e: e.activation(out=tcc[:], in_=tcc[:], func=AF.Ln, bias=1.0), reads=["tc"], writes=["tc"])
                                op("act", lambda e: e.activation(out=tb_[:], in_=tb_[:], func=AF.Exp, scale=-1.0), reads=["tb"], writes=["tb"])
                                yield
                                op("act", lambda e: e.activation(out=tcc[:], in_=tcc[:], func=AF.Exp, scale=-1.0), reads=["tc"], writes=["tc"])
                                op("act", lambda e: e.activation(out=td[:], in_=tb_[:], func=AF.Exp, scale=cvd[:, cg * 4 + 2:cg * 4 + 3]),
                                   reads=["tb", "cvd"], writes=["td"])
                                yield
                                op("act", lambda e: e.activation(out=te[:], in_=tb_[:], func=AF.Exp, scale=cvd[:, cg * 4 + 3:cg * 4 + 4]),
                                   reads=["tb", "cvd"], writes=["te"])
                                op("dve", lambda e: e.tensor_mul(out=tcc[:], in0=tcc[:], in1=ta[:]), reads=["tc", "ta"], writes=["tc"])
                                yield
                                op("act", lambda e: e.activation(out=te[:], in_=te[:], func=AF.Ln, scale=-1.0, bias=1.0), reads=["te"], writes=["te"])
                                op("act", lambda e: e.activation(out=te[:], in_=te[:], func=AF.Exp, scale=0.5), reads=["te"], writes=["te"])
                                yield
                                op("dve", lambda e: e.tensor_mul(out=tcc[:], in0=tcc[:], in1=te[:]), reads=["tc", "te"], writes=["tc"])
                                yield
                                op("dve", lambda e: e.tensor_tensor_scan(out=tf[:], data0=td[:], data1=tcc[:], initial=hcar[:, cg:cg + 1],
                                                                         op0=ALU.mult, op1=ALU.add),
                                   reads=["td", "tc", "hcar"], writes=["tf"])
                                op("dve", lambda e: e.tensor_copy(out=hcar[:, cg:cg + 1], in_=tf[:, TT - 1:TT]), reads=["tf"], writes=["hcar"])
                                yield
                                while not gr_ready[cg]:
                                    yield "blocked"
                                op("pool", lambda e: e.tensor_mul(out=ysl[par][:, cg, :], in0=tf[:], in1=Gr[:, cg, :]), reads=["tf", ("Gr", cg)], writes=[("ysl", par, cg)])
                                op("pool", lambda e: e.tensor_mul(out=sqb[:], in0=tf[:], in1=tf[:]), reads=["tf"], writes=["sqb"])
                                yield
                                for jb in range(4):
                                    col = cg * 4 + jb
                                    op("pe", lambda e: e.matmul(PSS[:, col:col + 1], lhsT=sqb[:, jb * 128:(jb + 1) * 128], rhs=ones_b[:, 0:1],
                                                                start=True, stop=True), reads=["sqb", "ones_b"], writes=["pss"])
                                yield
                            op("dve", lambda e: e.tensor_copy(out=pss_sb[:, 0:8], in_=PSS[:, 0:8]), reads=["pss"], writes=["pss_l"])
                            op("pool", lambda e: e.tensor_add(out=ss[:, g * 2, i * 4:(i + 1) * 4], in0=pss_sb[:, 0:4], in1=pss_sb[:, 4:8]),
                               reads=["pss_l"], writes=[("ss", g, 0)])
                            sy.dma("ysl%d" % par, lambda e: e.dma_start(out=yT[g * 4:g * 4 + 2, :, i * TT:(i + 1) * TT].rearrange("c p t -> p c t"), in_=ysl[par][:]),
                                   reads=[("ysl", par, 0), ("ysl", par, 1)], writes=[("yTl", g, i)])
                            yield

                        A, B = chain_a(), chain_b()
                        doneA = doneB = False
                        END = object()
                        while not (doneA and doneB):
                            if not doneB:
                                if next(B, END) is END:
                                    doneB = True
                                else:
                                    yield
                            if not doneA:
                                r = next(A, END)
                                if r is END:
                                    doneA = True
                                elif r != "blocked":
                                    yield

                    def attention(i, pump):
                        par = i % 2
                        steps = []
                        for h in range(NH):
                            hs = []
                            for kt in range(4 * i):
                                hs.append(("full", h, kt))
                            for j in range(4):
                                hs.append(("diag", h, j))
                            for n_, s_ in enumerate(hs):
                                steps.append(s_ + (n_ == 0, n_ == len(hs) - 1))

                        deferred = []

                        def run_deferred(force=False):
                            k = 0
                            while k < len(deferred):
                                deferred[k][0] -= 1
                                if deferred[k][0] <= 0 or force:
                                    deferred.pop(k)[1]()
                                else:
                                    k += 1

                        def emit_qk(si, st):
                            kind, h, a, first, last = st
                            sb = si % 4
                            if kind == "full":
                                kt = a
                                op("pe", lambda e: e.matmul(PSA[:, sb, :], lhsT=Kc[h][0:70, kt * 128:(kt + 1) * 128], rhs=Qa[par][h][0:70, :],
                                                            start=True, stop=True),
                                   reads=[("K", h, kt // 4), ("Kaug", h, kt // 4), ("Kones", h), ("Qa", par, h)], writes=[bk(sb)])
                            else:
                                j = a
                                kt = i * 4 + j
                                op("pe", lambda e: e.matmul(PSA[:, sb, j * 128:TT], lhsT=Kc[h][0:70, kt * 128:(kt + 1) * 128], rhs=Qa[par][h][0:70, j * 128:TT],
                                                            start=True, stop=False),
                                   reads=[("K", h, i), ("Kaug", h, i), ("Kones", h), ("Qa", par, h)], writes=[bk(sb)])
                                op("pe", lambda e: e.matmul(PSA[:, sb, j * 128:(j + 1) * 128], lhsT=ident[:], rhs=tri[:], start=False, stop=True),
                                   reads=["ident", "tri"], writes=[bk(sb)])

                        def emit_exp(si, st):
                            kind, h, a, first, last = st
                            sb = si % 4
                            c0 = 0 if kind == "full" else a * 128
                            op("act", lambda e: e.activation(out=PT[sb][:, c0:TT], in_=PSA[:, sb, c0:TT], func=AF.Exp),
                               reads=[bk(sb)], writes=[("PT", sb)])

                        def emit_pv(si, st):
                            kind, h, a, first, last = st
                            sb = si % 4
                            ob = 4 + (h % 2)
                            if kind == "full":
                                kt, c0 = a, 0
                                vkey = ("V", kt // 4)
                            else:
                                kt, c0 = i * 4 + a, a * 128
                                vkey = ("V", i)
                            op("pe", lambda e: e.matmul(PSA[0:DH + 1, ob, c0:TT], lhsT=Vc[:, kt, h, :], rhs=PT[sb][:, c0:TT],
                                                        start=first, stop=last),
                               reads=[vkey, "Vones", ("PT", sb)], writes=[bk(ob)])
                            if last:
                                finalize(h)

                        def finalize(h):
                            ob = 4 + (h % 2)
                            hp, s_ = h // 2, h % 2
                            po = s_ * 64

                            def st1():
                                op("act", lambda e: e.activation(out=fin1[64:65, :], in_=PSA[64:65, ob, :], func=AF.Ln), reads=[bk(ob)], writes=["rinv"])
                                op("act", lambda e: e.activation(out=fin1[64:65, :], in_=fin1[64:65, :], func=AF.Exp, scale=-1.0), reads=["rinv"], writes=["rinv"])
                                deferred.append([2, st2])

                            def st2():
                                op("pe", lambda e: e.matmul(MISC[0:64, :], lhsT=ones_f[64:65, 0:64], rhs=fin1[64:65, :], start=True, stop=True),
                                   reads=["rinv", "ones_f"], writes=["misc"])
                                op("dve", lambda e: e.tensor_copy(out=fin1[0:64, :], in_=MISC[0:64, :]), reads=["misc"], writes=["bcs"])
                                deferred.append([1, st2b])

                            def st2b():
                                op("dve", lambda e: e.tensor_mul(out=fin2[po:po + 64, :], in0=PSA[0:64, ob, :], in1=fin1[0:64, :]), reads=[bk(ob), "bcs"], writes=["yn"])
                                op("pool" if s_ == 0 else "dve", lambda e: e.tensor_mul(out=sqa[0:64, :], in0=fin2[po:po + 64, :], in1=fin2[po:po + 64, :]),
                                   reads=["yn"], writes=["sqa"])
                                op("pool", lambda e: e.tensor_mul(out=ysa[po:po + 64, hp, :], in0=fin2[po:po + 64, :], in1=Ga[par][po:po + 64, hp, :]),
                                   reads=["yn", ("Ga", par, hp)], writes=[("ysa", hp, s_)])
                                deferred.append([2, st3])

                            def st3():
                                for jb in range(4):
                                    col = 8 + h * 4 + jb
                                    op("pe", lambda e: e.matmul(PSS[:, col:col + 1], lhsT=sqa[0:64, jb * 128:(jb + 1) * 128], rhs=ones_b[0:64, 0:1],
                                                                start=True, stop=True), reads=["sqa", "ones_b"], writes=["pss"])
                            deferred.append([1, st1])

                        for si, st in enumerate(steps):
                            emit_qk(si, st)
                            emit_exp(si, st)
                            if si > 2:
                                emit_pv(si - 3, steps[si - 3])
                            if si % 2 == 1:
                                run_deferred()
                            pump(len(steps) - si)
                        for k_ in (3, 2, 1):
                            emit_pv(len(steps) - k_, steps[-k_])
                        while deferred:
                            run_deferred(force=True)
                        op("dve", lambda e: e.tensor_copy(out=pss_sb[:, 8:24], in_=PSS[:, 8:24]), reads=["pss"], writes=["pss_a"])
                        op("pool", lambda e: e.tensor_add(out=pss_sb[:, 8:12], in0=pss_sb[:, 8:12], in1=pss_sb[:, 12:16]), reads=["pss_a"], writes=["pss_a"])
                        op("pool", lambda e: e.tensor_add(out=pss_sb[:, 16:20], in0=pss_sb[:, 16:20], in1=pss_sb[:, 20:24]), reads=["pss_a"], writes=["pss_a"])
                        op("pool", lambda e: e.tensor_add(out=ss[:, g * 2 + 1, i * 4:(i + 1) * 4], in0=pss_sb[:, 8:12], in1=pss_sb[:, 16:20]),
                           reads=["pss_a"], writes=[("ss", g, 1)])
                        sy.dma("ysa", lambda e: e.dma_start(out=yT[g * 4 + 2:g * 4 + 4, :, i * TT:(i + 1) * TT].rearrange("c p t -> p c t"), in_=ysa[:]),
                               reads=[("ysa", 0, 0), ("ysa", 0, 1), ("ysa", 1, 0), ("ysa", 1, 1)], writes=[("yTa", g, i)])

                    n_yield = [0]
                    g0 = pre(0)
                    for _ in g0:
                        n_yield[0] += 1
                    for i in range(NT):
                        nxt = pre(i + 1) if i + 1 < NT else None
                        left = [n_yield[0]]

                        n_steps_i = NH * (4 * i + 4)
                        stride = max(1, (n_steps_i - 4) // max(1, n_yield[0]))

                        def pump(steps_left):
                            if nxt is None or left[0] <= 0:
                                return
                            if stride > 1 and (steps_left % stride) != 0:
                                return
                            k = -(-left[0] // max(1, (steps_left - 2) // stride))
                            for _ in range(k):
                                if left[0] <= 0:
                                    break
                                left[0] -= 1
                                try:
                                    next(nxt)
                                except StopIteration:
                                    left[0] = 0
                        attention(i, pump)
                        if nxt is not None:
                            for _ in nxt:
                                pass
                    sy.barrier()

            with ExitStack() as ph:
                Wo = T("Wo", [128, NCH, D], BF16, stack=ph)
                wos = [T("wos%d" % i, [128, D], stack=ph) for i in range(2)]
                g_bc = T("g_bc", [128, D], stack=ph)
                b_bc = T("b_bc", [128, D], stack=ph)
                ybuf = [T("ybuf%d" % i, [128, NCH, TT], BF16, stack=ph) for i in range(2)]
                xo = [T("xo%d" % i, [128, D], stack=ph) for i in range(2)]
                t1s = [T("t1_%d" % i, [128, D], stack=ph) for i in range(2)]
                ress = [T("res%d" % i, [128, D], stack=ph) for i in range(4)]
                xo2 = [T("xo2_%d" % i, [128, D], stack=ph) for i in range(2)]
                rr = T("rr", [128, 2, NB], stack=ph)
                st6 = T("st6o", [128, 4, 2, 6], stack=ph)
                mv = T("mvo", [128, 4, 2], stack=ph)
                rstd = T("rstdo", [128, 4], stack=ph)
                nmr = T("nmr", [128, 4], stack=ph)

                sy.dma("gb", lambda e: e.dma_start(out=g_bc[:], in_=ln_g[L, :, :].partition_broadcast(128)), writes=["g_bc"])
                sy.dma("gb2", lambda e: e.dma_start(out=b_bc[:], in_=ln_b[L, :, :].partition_broadcast(128)), writes=["b_bc"])
                for c in range(NCH):
                    buf = wos[c % 2]
                    key = "wos%d" % (c % 2)
                    sy.dma(key, lambda e: e.dma_start(out=buf[:], in_=w_out[L, c, :, :]), writes=[key])
                    op("dve" if c % 2 == 0 else "pool", lambda e: e.tensor_mul(out=Wo[:, c, :], in0=buf[:], in1=gate_bc[:]),
                       reads=[key, "gate_bc"], writes=[("Wo", c)])
                for br in range(2):
                    op("pool", lambda e: e.tensor_add(out=rr[:, br, :], in0=ss[:, br, :], in1=ss[:, 2 + br, :]),
                       reads=[("ss", 0, br), ("ss", 1, br)], writes=[("rr", br)])
                    op("act", lambda e: e.activation(out=rr[:, br, :], in_=rr[:, br, :], func=AF.Ln, scale=1.0 / 512.0, bias=EPS),
                       reads=[("rr", br)], writes=[("rr", br)])
                    op("act", lambda e: e.activation(out=rr[:, br, :], in_=rr[:, br, :], func=AF.Exp, scale=-0.5),
                       reads=[("rr", br)], writes=[("rr", br)])
                chunks = {0: (0, 1, 4, 5), 1: (2, 3, 6, 7)}
                def load_y(i):
                    yb = ybuf[i % 2]
                    ykey = "ybuf%d" % (i % 2)
                    sy.dma(ykey, lambda e: e.dma_start(out=yb[:], in_=yT[:, :, i * TT:(i + 1) * TT].rearrange("c p t -> p c t")),
                           reads=[("yTl", 0, i), ("yTl", 1, i), ("yTa", 0, i), ("yTa", 1, i)], writes=[ykey])

                def load_x(tbk):
                    xt = xo[tbk % 2]
                    xk = "xo%d" % (tbk % 2)
                    sy.dma(xk, lambda e: e.dma_start(out=xt[:], in_=x_src[tbk * 128:(tbk + 1) * 128, :]), reads=[(xs_key, tbk)], writes=[xk])

                def stA(tbk):
                    i, jb = tbk // 4, tbk % 4
                    yb = ybuf[i % 2]
                    ykey = "ybuf%d" % (i % 2)
                    if tbk == 0:
                        load_y(0)
                        load_x(0)
                    if jb == 0 and i + 1 < NT:
                        load_y(i + 1)
                    if tbk + 1 < NB:
                        load_x(tbk + 1)
                    xt = xo[tbk % 2]
                    xk = "xo%d" % (tbk % 2)
                    t1 = t1s[tbk % 2]
                    t1k = "t1_%d" % (tbk % 2)
                    res = ress[tbk % 4]
                    rk = "res%d" % (tbk % 4)
                    for nb in range(2):
                        for br in range(2):
                            bank = nb * 2 + br
                            for n_, c in enumerate(chunks[br]):
                                op("pe", lambda e: e.matmul(PSA[:, bank, :], lhsT=yb[:, c, jb * 128:(jb + 1) * 128], rhs=Wo[:, c, nb * 512:(nb + 1) * 512],
                                                            start=(n_ == 0), stop=(n_ == 3)),
                                   reads=[ykey, ("Wo", c)], writes=[bk(bank)])
                    for nb in range(2):
                        op("act", lambda e: e.activation(out=t1[:, nb * 512:(nb + 1) * 512], in_=PSA[:, nb * 2, :], func=AF.Copy, scale=rr[:, 0, tbk:tbk + 1]),
                           reads=[bk(nb * 2), ("rr", 0)], writes=[(t1k, nb)])
                        op("dve", lambda e: e.scalar_tensor_tensor(out=t1[:, nb * 512:(nb + 1) * 512], in0=PSA[:, nb * 2 + 1, :], scalar=rr[:, 1, tbk:tbk + 1],
                                                                   in1=t1[:, nb * 512:(nb + 1) * 512], op0=ALU.mult, op1=ALU.add),
                           reads=[bk(nb * 2 + 1), ("rr", 1), (t1k, nb)], writes=[(t1k, nb)])
                    op("dve", lambda e: e.scalar_tensor_tensor(out=res[:], in0=xt[:], scalar=alpha, in1=t1[:], op0=ALU.mult, op1=ALU.add),
                       reads=[xk, (t1k, 0), (t1k, 1)], writes=[rk])

                def stA2(tbk):
                    q3 = tbk % 4
                    res = ress[q3]
                    rk = "res%d" % q3
                    for hh in range(2):
                        op("dve", lambda e: e.bn_stats(out=st6[:, q3, hh, :], in_=res[:, hh * 512:(hh + 1) * 512]), reads=[rk], writes=[("st6o", q3, hh)])
                    op("dve", lambda e: e.bn_aggr(out=mv[:, q3, :], in_=st6[:, q3, :, :]), reads=[("st6o", q3, 0), ("st6o", q3, 1)], writes=[("mvo", q3)])

                def stB(tbk):
                    q3 = tbk % 4
                    res = ress[q3]
                    rk = "res%d" % q3
                    op("act", lambda e: e.activation(out=rstd[:, q3:q3 + 1], in_=mv[:, q3, 1:2], func=AF.Ln, bias=EPS), reads=[("mvo", q3)], writes=[("rstdo", q3)])
                    op("act", lambda e: e.activation(out=rstd[:, q3:q3 + 1], in_=rstd[:, q3:q3 + 1], func=AF.Exp, scale=-0.5), reads=[("rstdo", q3)], writes=[("rstdo", q3)])
                    op("dve", lambda e: e.scalar_tensor_tensor(out=nmr[:, q3:q3 + 1], in0=mv[:, q3, 0:1], scalar=-1.0, in1=rstd[:, q3:q3 + 1], op0=ALU.mult, op1=ALU.mult),
                       reads=[("mvo", q3), ("rstdo", q3)], writes=[("nmr", q3)])
                    op("act", lambda e: e.activation(out=res[:], in_=res[:], func=AF.Identity, scale=rstd[:, q3:q3 + 1], bias=nmr[:, q3:q3 + 1]),
                       reads=[rk, ("rstdo", q3), ("nmr", q3)], writes=[rk])

                def stC(tbk):
                    res = ress[tbk % 4]
                    rk = "res%d" % (tbk % 4)
                    xw = xo2[tbk % 2]
                    xwk = "xo2_%d" % (tbk % 2)
                    op("pool", lambda e: e.tensor_mul(out=res[:], in0=res[:], in1=g_bc[:]), reads=[rk, "g_bc"], writes=[rk])
                    op("pool", lambda e: e.tensor_add(out=xw[:], in0=res[:], in1=b_bc[:]), reads=[rk, "b_bc"], writes=[xwk])
                    sy.dma(xwk + "s", lambda e: e.dma_start(out=x_dst[tbk * 128:(tbk + 1) * 128, :], in_=xw[:]), reads=[xwk], writes=[(xd_key, tbk)])

                for n in range(NB + 3):
                    if n < NB:
                        stA(n)
                    if 0 <= n - 1 < NB:
                        stA2(n - 1)
                    if 0 <= n - 2 < NB:
                        stB(n - 2)
                    if 0 <= n - 3 < NB:
                        stC(n - 3)
                sy.barrier()
        sy.barrier()
    return nc


def make_inputs(inputs, S, NL):
    f = lambda a: np.ascontiguousarray(np.asarray(a, dtype=np.float32))
    w_in = f(inputs["w_in"])
    win_g = np.zeros((NL, 2, D, NCOL), np.float32)
    for g in range(2):
        sl = slice(g * 256, (g + 1) * 256)
        win_g[:, g, :, C_Q:C_Q + 256] = w_in[:NL, :, 1024:1536][:, :, sl]
        win_g[:, g, :, C_K:C_K + 256] = w_in[:NL, :, 1536:2048][:, :, sl]
        win_g[:, g, :, C_XR:C_XR + 256] = w_in[:NL, :, 0:512][:, :, sl]
        win_g[:, g, :, C_ZR:C_ZR + 256] = w_in[:NL, :, 512:1024][:, :, sl]
        win_g[:, g, :, C_ZA:C_ZA + 256] = w_in[:NL, :, 2560:3072][:, :, sl]
        win_g[:, g, :, C_V:C_V + 256] = w_in[:NL, :, 2048:2560][:, :, sl]
        win_g[:, g, :, C_FL:C_FL + 4] = w_in[:NL, :, 3072 + g * 4:3072 + (g + 1) * 4]
    conv_w, conv_b = f(inputs["conv_w"]), f(inputs["conv_b"])
    b_a, b_x, lam = f(inputs["b_gate_a"]), f(inputs["b_gate_x"]), f(inputs["lru_lambda"])
    n_l, n_a = f(inputs["norm_lru"]), f(inputs["norm_attn"])
    chanvec = np.zeros((NL, 2, 128, 20), np.float32)
    wga, wgx = f(inputs["w_gate_a"]), f(inputs["w_gate_x"])
    w_gate = np.zeros((NL, 2, 4, 128, 128), np.float32)
    for g in range(2):
        for cg in range(2):
            ch = slice(g * 256 + cg * 128, g * 256 + (cg + 1) * 128)
            o = cg * 9
            for k in range(4):
                chanvec[:, g, :, o + k] = conv_w[:NL, k, ch]
            chanvec[:, g, :, o + 4] = conv_b[:NL, ch]
            chanvec[:, g, :, o + 5] = b_a[:NL, ch]
            chanvec[:, g, :, o + 6] = b_x[:NL, ch]
            chanvec[:, g, :, o + 7] = lam[:NL, ch]
            chanvec[:, g, :, o + 8] = n_l[:NL, ch]
            chanvec[:, g, :, 18 + cg] = n_a[:NL, ch]
            for bl in range(2):
                blk = (g * 256 + cg * 128) // 64 + bl
                w_gate[:, g, cg * 2 + 0, bl * 64:(bl + 1) * 64, bl * 64:(bl + 1) * 64] = wga[:NL, blk]
                w_gate[:, g, cg * 2 + 1, bl * 64:(bl + 1) * 64, bl * 64:(bl + 1) * 64] = wgx[:NL, blk]
    w_out = f(inputs["w_out"])
    wout_r = np.zeros((NL, NCH, 128, D), np.float32)
    for g in range(2):
        for j in range(4):
            if j < 2:
                r0 = g * 256 + j * 128
            else:
                r0 = 512 + g * 256 + (j - 2) * 128
            wout_r[:, g * 4 + j] = w_out[:NL, r0:r0 + 128, :]
    b_fg = f(inputs["b_fgate"])[:NL].reshape(NL, 2, NH, 1)
    common = {
        "w_ada": f(inputs["w_ada"])[:NL],
        "b_ada": f(inputs["b_ada"])[:NL].reshape(NL, 1, 3 * D),
        "w_in": win_g, "b_fg": np.ascontiguousarray(b_fg), "chanvec": chanvec, "w_gate": w_gate, "w_out": wout_r,
        "ln_g": f(inputs["ln_gain"])[:NL].reshape(NL, 1, D), "ln_b": f(inputs["ln_bias"])[:NL].reshape(NL, 1, D),
        "ident": np.eye(128, dtype=np.float32),
        "tri": np.where(np.arange(128)[:, None] <= np.arange(128)[None, :], 0.0, NEG).astype(np.float32),
    }
    x = f(inputs["x"])
    c = f(inputs["c"])
    maps = []
    for b in range(x.shape[0]):
        m = dict(common)
        m["x"] = np.ascontiguousarray(x[b, :S])
        m["c_col"] = np.ascontiguousarray(c[b].reshape(NCH, 128).T)
        maps.append(m)
    return maps


_CACHE = {}


def run(inputs, S=8192, NL=2):
    key = (S, NL)
    if key not in _CACHE:
        _CACHE[key] = build_program(S, NL)
    nc = _CACHE[key]
    maps = make_inputs(inputs, S, NL)
    res = run_bass_kernel_spmd(nc, maps, core_ids=list(range(len(maps))))
    return np.stack([r["y"] for r in res.results], axis=0)


def kernel(**inputs):
    return run(inputs, S=8192, NL=2).astype(np.float32)
```
